# Optimizing a Trainium2 kernel written in Bass

```python
import jax, jax.numpy as jnp
from jax import lax
import numpy as np

D_MODEL = 1024
BATCH = 16
SEQ = 2048
DEPTH = 1

CTX_LEN = 256
GRID_W = 64
D_CONV = 1024
CONV_W = 3
N_HEADS = 8
DK = 64
DV = 128
D_MLSTM = N_HEADS * DV
D_FF = 4 * D_MODEL
CHUNK = 128
N_DIR = 2
N_BRANCH = 2
EPS = 1e-6

STATE_SPLITS = (N_HEADS * DK, D_MLSTM, N_DIR * N_HEADS, N_DIR * N_HEADS)
REST_SPLITS = (N_HEADS * DK, D_MLSTM, D_CONV, D_CONV, D_CONV, N_BRANCH * D_MODEL)
COL_SPLITS = STATE_SPLITS + REST_SPLITS
D_IN = sum(COL_SPLITS)
N_STATE_COLS = sum(STATE_SPLITS)
COL_OFFSETS = tuple(np.cumsum(COL_SPLITS)[:-1].tolist())
STATE_OFFSETS = tuple(np.cumsum(STATE_SPLITS)[:-1].tolist())
F_GATE_OFFSET = N_HEADS * DK + D_MLSTM + N_DIR * N_HEADS

kernel_name = "hybrid_conv_mlstm_prefix_dit"


def rmsnorm(x, w):
    xf = x.astype(jnp.float32)
    y = xf * lax.rsqrt(jnp.mean(xf * xf, axis=-1, keepdims=True) + EPS)
    return y.astype(x.dtype) * w


def modulate(x, w, shift, scale):
    return rmsnorm(x, w) * (1 + scale) + shift


def adaln(cvec, w_mod, b_mod):
    return jnp.split(jax.nn.silu(cvec) @ w_mod + b_mod, 6, axis=-1)


def ffn(h, w1, w2):
    return jnp.square(jax.nn.relu(h @ w1)) @ w2


def conv3(u, w, axis):
    n = u.shape[axis]
    pad = [(0, 0)] * u.ndim
    pad[axis] = (1, 1)
    p = jnp.pad(u, pad)
    return (w[0] * lax.slice_in_dim(p, 0, n, axis=axis)
            + w[1] * lax.slice_in_dim(p, 1, n + 1, axis=axis)
            + w[2] * lax.slice_in_dim(p, 2, n + 2, axis=axis))


def heads(a, d):
    b, t, _ = a.shape
    return a.reshape(b, t, -1, d).transpose(0, 2, 1, 3)


def dir_shared(a):
    return jnp.stack([a, a[..., ::-1, :]])


def dir_gates(g):
    b, t, _ = g.shape
    g = g.reshape(b, t, N_DIR, N_HEADS).transpose(2, 0, 3, 1)
    return jnp.stack([g[0], g[1][..., ::-1]])


def mlstm_dir_inputs(k, v, ig, fg):
    kd = dir_shared(heads(k, DK)).astype(jnp.float32) * (DK ** -0.5)
    vd = dir_shared(heads(v, DV)).astype(jnp.float32)
    igd = dir_gates(ig).astype(jnp.float32)
    lfd = jax.nn.log_sigmoid(dir_gates(fg).astype(jnp.float32))
    return kd, vd, igd, lfd


def mlstm_chunked(q, k, v, ig, lf, C0, n0, m0):
    lead = q.shape[:-2]
    t = q.shape[-2]
    nc = t // CHUNK

    def feat_chunks(a):
        return jnp.moveaxis(a.reshape(*lead, nc, CHUNK, a.shape[-1]), -3, 0)

    def gate_chunks(a):
        return jnp.moveaxis(a.reshape(*lead, nc, CHUNK), -2, 0)

    causal_in_chunk = jnp.tril(jnp.ones((CHUNK, CHUNK), dtype=bool))

    def step(carry, inp):
        C, n, m = carry
        qc, kc, vc, ic, fc = inp
        b = jnp.cumsum(fc, axis=-1)
        log_d = jnp.where(causal_in_chunk,
                          b[..., :, None] - b[..., None, :] + ic[..., None, :], -jnp.inf)
        inter = b + m[..., None]
        m_t = jnp.maximum(jnp.max(log_d, axis=-1), inter)
        s = jnp.einsum('...tk,...sk->...ts', qc, kc) * jnp.exp(log_d - m_t[..., None])
        w_inter = jnp.exp(inter - m_t)
        num = (jnp.einsum('...ts,...sv->...tv', s, vc)
               + w_inter[..., None] * jnp.einsum('...vk,...tk->...tv', C, qc))
        den = jnp.sum(s, axis=-1) + w_inter * jnp.einsum('...k,...tk->...t', n, qc)
        h = num / jnp.maximum(jnp.abs(den), jnp.exp(-m_t))[..., None]
        b_last = b[..., -1]
        g = b_last[..., None] - b + ic
        m_new = jnp.maximum(b_last + m, jnp.max(g, axis=-1))
        wk = jnp.exp(g - m_new[..., None])
        decay = jnp.exp(b_last + m - m_new)
        C_new = decay[..., None, None] * C + jnp.einsum('...s,...sv,...sk->...vk', wk, vc, kc)
        n_new = decay[..., None] * n + jnp.einsum('...s,...sk->...k', wk, kc)
        return (C_new, n_new, m_new), h

    state, hs = lax.scan(step, (C0, n0, m0),
                         (feat_chunks(q), feat_chunks(k), feat_chunks(v), gate_chunks(ig), gate_chunks(lf)))
    hs = jnp.moveaxis(hs, 0, -3).reshape(*lead, t, v.shape[-1])
    return hs, state


def mlstm_state(k, v, ig, lf):
    b = jnp.cumsum(lf, axis=-1)
    b_last = b[..., -1]
    g = b_last[..., None] - b + ig
    m = jnp.maximum(b_last, jnp.max(g, axis=-1))
    w = jnp.exp(g - m[..., None])
    C = jnp.einsum('...s,...sv,...sk->...vk', w, v, k)
    n = jnp.einsum('...s,...sk->...k', w, k)
    return C, n, m


def context_state(hc, w_in, b_in):
    z = hc @ w_in[:, :N_STATE_COLS] + b_in[:N_STATE_COLS]
    k, v, ig, fg = jnp.split(z, STATE_OFFSETS, axis=-1)
    return mlstm_state(*mlstm_dir_inputs(k, v, ig, fg))


def token_mixers(h, w_in, b_in, conv_w, mlstm_norm_w, w_conv_out, w_mlstm_out, w_out, state0, grid):
    bsz, t, _ = h.shape
    k, v, ig, fg, q, o, xin, gate_c, gate_b, merge = jnp.split(h @ w_in + b_in, COL_OFFSETS, axis=-1)
    u = gate_c * xin
    if grid:
        rows = t // GRID_W
        a = conv3(u.reshape(bsz, rows, GRID_W, D_CONV), conv_w, axis=2).reshape(bsz, t, D_CONV)
    else:
        a = conv3(u, conv_w, axis=1)
    y_a = (gate_b * a) @ w_conv_out
    kd, vd, igd, lfd = mlstm_dir_inputs(k, v, ig, fg)
    qd = dir_shared(heads(q, DK)).astype(jnp.float32)
    hd, state = mlstm_chunked(qd, kd, vd, igd, lfd, *state0)
    hs = (hd[0] + hd[1][..., ::-1, :]).transpose(0, 2, 1, 3)
    hs = hs * lax.rsqrt(jnp.mean(hs * hs, axis=-1, keepdims=True) + EPS)
    hs = hs.reshape(bsz, t, D_MLSTM).astype(h.dtype) * mlstm_norm_w
    y_b = (hs * jax.nn.sigmoid(o)) @ w_mlstm_out
    g_a, g_b = jnp.split(jax.nn.sigmoid(merge), N_BRANCH, axis=-1)
    return (g_a * y_a + g_b * y_b) @ w_out, state


def setup_inputs(seed: int = 0) -> dict:
    key = jax.random.key(seed)
    ks = jax.random.split(key, 20)

    def nrm(k, shape, s):
        return jax.random.normal(k, shape, jnp.float32) * s

    b_in = nrm(ks[7], (DEPTH, D_IN), 0.02)
    b_in = b_in.at[:, F_GATE_OFFSET:F_GATE_OFFSET + N_DIR * N_HEADS].add(
        jnp.tile(jnp.linspace(3.0, 6.0, N_HEADS), N_DIR))
    return {
        "x": nrm(ks[0], (BATCH, SEQ, D_MODEL), 1.0),
        "c": nrm(ks[1], (BATCH, D_MODEL), 1.0),
        "ctx": nrm(ks[2], (BATCH, CTX_LEN, D_MODEL), 1.0),
        "c_ctx": nrm(ks[3], (D_MODEL,), 1.0),
        "w_mod": nrm(ks[4], (DEPTH, D_MODEL, 6 * D_MODEL), 0.5 * D_MODEL ** -0.5),
        "b_mod": nrm(ks[5], (DEPTH, 6 * D_MODEL), 0.02),
        "norm1_w": 1.0 + nrm(ks[6], (DEPTH, D_MODEL), 0.02),
        "w_in": nrm(ks[8], (DEPTH, D_MODEL, D_IN), D_MODEL ** -0.5),
        "b_in": b_in,
        "conv_w": nrm(ks[9], (DEPTH, CONV_W, D_CONV), 0.5),
        "mlstm_norm_w": 1.0 + nrm(ks[10], (DEPTH, D_MLSTM), 0.02),
        "w_conv_out": nrm(ks[11], (DEPTH, D_CONV, D_MODEL), D_CONV ** -0.5),
        "w_mlstm_out": nrm(ks[12], (DEPTH, D_MLSTM, D_MODEL), D_MLSTM ** -0.5),
        "w_out": nrm(ks[13], (DEPTH, D_MODEL, D_MODEL), D_MODEL ** -0.5),
        "norm2_w": 1.0 + nrm(ks[14], (DEPTH, D_MODEL), 0.02),
        "w_ff1": nrm(ks[15], (DEPTH, D_MODEL, D_FF), D_MODEL ** -0.5),
        "w_ff2": nrm(ks[16], (DEPTH, D_FF, D_MODEL), D_FF ** -0.5),
        "final_norm_w": 1.0 + nrm(ks[17], (D_MODEL,), 0.02),
    }


def reference(x, c, ctx, c_ctx, w_mod, b_mod, norm1_w, w_in, b_in, conv_w, mlstm_norm_w,
              w_conv_out, w_mlstm_out, w_out, norm2_w, w_ff1, w_ff2, final_norm_w):
    for l in range(DEPTH):
        sh1, sc1, g1, sh2, sc2, g2 = [m[:, None, :] for m in adaln(c, w_mod[l], b_mod[l])]
        csh1, csc1, cg1, csh2, csc2, cg2 = adaln(c_ctx, w_mod[l], b_mod[l])
        hc = modulate(ctx, norm1_w[l], csh1, csc1)
        if l + 1 < DEPTH:
            bsz = ctx.shape[0]
            zero_state = (jnp.zeros((N_DIR, bsz, N_HEADS, DV, DK), jnp.float32),
                          jnp.zeros((N_DIR, bsz, N_HEADS, DK), jnp.float32),
                          jnp.zeros((N_DIR, bsz, N_HEADS), jnp.float32))
            out_c, state = token_mixers(hc, w_in[l], b_in[l], conv_w[l], mlstm_norm_w[l], w_conv_out[l],
                                        w_mlstm_out[l], w_out[l], zero_state, grid=False)
            ctx_next = ctx + cg1 * out_c
            ctx_next = ctx_next + cg2 * ffn(modulate(ctx_next, norm2_w[l], csh2, csc2), w_ff1[l], w_ff2[l])
        else:
            state = context_state(hc, w_in[l], b_in[l])
            ctx_next = ctx
        h = modulate(x, norm1_w[l], sh1, sc1)
        out, _ = token_mixers(h, w_in[l], b_in[l], conv_w[l], mlstm_norm_w[l], w_conv_out[l],
                              w_mlstm_out[l], w_out[l], state, grid=True)
        x = x + g1 * out
        x = x + g2 * ffn(modulate(x, norm2_w[l], sh2, sc2), w_ff1[l], w_ff2[l])
        ctx = ctx_next
    return rmsnorm(x, final_norm_w)
```

```python
import math
from contextlib import ExitStack

import numpy as np
import concourse.bass as bass
import concourse.mybir as mybir
from concourse.bass_utils import run_bass_kernel_spmd

F32 = mybir.dt.float32
BF16 = mybir.dt.bfloat16
AF = mybir.ActivationFunctionType
ALU = mybir.AluOpType
AX = mybir.AxisListType

D = 1024
T = 2048
CT = 256
NT = 16
NTC = 18
H = 8
EPS = 1e-6
N_CORES = 8
NQ = 8
FQ = 32 // NQ

ENGS = ("pe", "act", "dve", "pool", "sp")
N_DMA_SEMS = 8
import os
KVAR = int(os.environ.get('KVAR', '0'))


class Op:
    __slots__ = ("eng", "fn", "dma", "deps", "signal", "sig_sem", "sig_val", "idx")

    def __init__(self, eng, fn, dma):
        self.eng = eng
        self.fn = fn
        self.dma = dma
        self.deps = set()
        self.signal = False
        self.sig_sem = None
        self.sig_val = 0


class Sched:
    def __init__(self):
        self.ops = []
        self.eops = {e: [] for e in ENGS}
        self.last_w = {}
        self.readers = {}
        self.epoch_op = None
        self.dma_since = []

    def barrier(self, eng, fn):
        o = Op(eng, fn, False)
        o.idx = len(self.ops)
        for e in ENGS:
            for p in reversed(self.eops[e]):
                if not p.dma:
                    o.deps.add(p)
                    break
        for p in self.dma_since:
            o.deps.add(p)
        if self.epoch_op is not None:
            o.deps.add(self.epoch_op)
        self.dma_since = []
        self.epoch_op = o
        self.ops.append(o)
        self.eops[eng].append(o)
        return o

    def op(self, eng, fn, reads=(), writes=(), dma=False, epoch=True):
        o = Op(eng, fn, dma)
        o.idx = len(self.ops)
        reads = list(reads)
        if epoch and self.epoch_op is not None:
            o.deps.add(self.epoch_op)
        if dma:
            self.dma_since.append(o)
        for k in reads:
            w = self.last_w.get(k)
            if w is not None:
                o.deps.add(w)
        for k in writes:
            w = self.last_w.get(k)
            if w is not None:
                o.deps.add(w)
            for r in self.readers.get(k, {}).values():
                o.deps.add(r)
        for k in reads:
            self.readers.setdefault(k, {})[("dma", o.idx) if dma else eng] = o
        for k in writes:
            self.last_w[k] = o
            self.readers[k] = {}
        o.deps.discard(o)
        self.ops.append(o)
        self.eops[eng].append(o)
        return o

    def emit(self, block_engines, sems, dma_sems):
        for o in self.ops:
            nd = set()
            for d in o.deps:
                if (not d.dma) and (not o.dma) and d.eng == "pe" and o.eng == "pe":
                    continue
                nd.add(d)
                d.signal = True
            o.deps = nd
        cnt = {e: 0 for e in ENGS}
        dcnt = {e: 0 for e in ENGS}
        dsem_val = {}
        prev_on_sem = {}
        for e in ENGS:
            for o in self.eops[e]:
                if o.dma:
                    i = dcnt[e] % N_DMA_SEMS
                    dcnt[e] += 1
                    dsem_val[(e, i)] = dsem_val.get((e, i), 0) + 16
                    o.sig_sem = dma_sems[e][i]
                    o.sig_val = dsem_val[(e, i)]
                    p = prev_on_sem.get((e, i))
                    if p is not None:
                        o.deps.add(p)
                    prev_on_sem[(e, i)] = o
                    o.signal = True
                elif o.signal:
                    cnt[e] += 1
                    o.sig_sem = sems[e]
                    o.sig_val = cnt[e]
        finals = list(prev_on_sem.values())

        def run(eng_name):
            def body(eng):
                waited = {}
                for o in self.eops[eng_name]:
                    need = {}
                    for d in o.deps:
                        k = id(d.sig_sem)
                        if k not in need or need[k][1] < d.sig_val:
                            need[k] = (d.sig_sem, d.sig_val)
                    for k, (s, v) in need.items():
                        if waited.get(k, 0) >= v:
                            continue
                        eng.wait_ge(s, v)
                        waited[k] = v
                    ins = o.fn(eng)
                    if o.signal:
                        ins.then_inc(o.sig_sem, 16 if o.dma else 1)
                if eng_name == "sp":
                    for o in finals:
                        k = id(o.sig_sem)
                        if waited.get(k, 0) >= o.sig_val:
                            continue
                        eng.wait_ge(o.sig_sem, o.sig_val)
                        waited[k] = o.sig_val
            return body

        for e in ENGS:
            block_engines[e](run(e))


class _Stop(Exception):
    pass


def build_program(NB, kstop=99):
    nc = bass.Bass("TRN2", target_bir_lowering=False)

    def dram(name, shape, kind="ExternalInput"):
        return nc.dram_tensor(name, list(shape), F32, kind=kind).ap()

    x_d = dram("x", [NB, T, D])
    ctx_d = dram("ctx", [NB, CT, D])
    cvecF_d = dram("cvecF", [128, 8, 3])
    wmod_d = dram("wmod", [D, 6 * D])
    bmodF_d = dram("bmodF", [128, 48])
    bmod_d = dram("bmod", [6 * D])
    n1wF_d = dram("n1wF", [128, 8])
    n2wF_d = dram("n2wF", [128, 8])
    wg_d = dram("wg", [D, 32])
    wtm_d = dram("wtm", [D, 8 * 384])
    wcv_d = dram("wcv", [D, 8 * 384])
    wpost_d = dram("wpost", [D, 8 * 512])
    bg_d = dram("bg", [32])
    btm_d = dram("btm", [8 * 384])
    bcvF_d = dram("bcvF", [128, 24])
    bmgF_d = dram("bmgF", [128, 16])
    cwF_d = dram("cwF", [128, 8, 3])
    mnw_d = dram("mnw", [D])
    wout_d = dram("wout", [D, D])
    wff1_d = dram("wff1", [D, 4 * D])
    wff2_d = dram("wff2", [4 * D, D])
    fnw_d = dram("fnw", [D])
    out_d = dram("out", [NB, T, D], kind="ExternalOutput")

    S = Sched()
    with ExitStack() as es:
        def sb(name, shape, dt):
            return es.enter_context(nc.sbuf_tensor(name, list(shape), dt))

        def pst(name, shape, dt):
            return es.enter_context(nc.psum_tensor(name, list(shape), dt))

        BIGB = 143360
        BIG = sb("BIG", [128, BIGB // 2], BF16)

        def bview(off_bytes, nbytes, dt=BF16):
            v = BIG[:, off_bytes // 2:(off_bytes + nbytes) // 2]
            if dt == F32:
                v = v.bitcast(F32)
            return v

        hT = bview(0, 36864).rearrange("p (k t) -> p k t", k=8)
        hsT = bview(36864, 32768).rearrange("p (k t) -> p k t", k=8)
        aT = bview(69632, 32768).rearrange("p (k t) -> p k t", k=8)
        mT = bview(102400, 32768).rearrange("p (k t) -> p k t", k=8)
        SCR = 135168
        A0 = 69632
        TM = bview(A0, 18 * 386 * 2).rearrange("p (c w) -> p c w", w=386)
        o1 = A0 + 18 * 386 * 2
        SCL = bview(o1, 9216).rearrange("p (c a b) -> p c a b", a=4, b=64)
        KH = bview(o1 + 9216, 4608).rearrange("p (c a b) -> p c a b", a=2, b=64)
        DC = bview(o1 + 13824, 9288, F32).rearrange("p (c w) -> p c w", w=129)
        HN = bview(o1, 16512, F32).rearrange("p (c a w) -> p c a w", a=2, w=129)
        TO = bview(o1 + 16512, 4096).rearrange("p (c w) -> p c w", w=128)
        o2 = o1 + 23112
        KQT = bview(o2, 8192).rearrange("p (c a t) -> p c a t", a=2, t=128)
        CSf = bview(o2 + 8192, 9288, F32).rearrange("p (c w) -> p c w", w=129)
        CS = bview(o2 + 17480, 4160).rearrange("p (c w) -> p c w", w=130)
        HS = bview(o2, 8192, F32).rearrange("p (c w) -> p c w", w=128)
        SQ = bview(o2 + 8192, 8192, F32).rearrange("p (c w) -> p c w", w=128)
        GT = bview(o2 + 16384, 4096).rearrange("p (c w) -> p c w", w=128)
        o3 = o2 + 21640
        PP = [bview(o3 + i * 1024, 1024).rearrange("p (a t) -> p a t", a=4) for i in range(4)]
        assert o3 + 4096 <= 135168
        X3 = [bview(102400 + i * 2048, 2048, F32) for i in range(3)]
        CU = bview(102400 + 6144, 2048, F32)
        CA = bview(102400 + 8192, 2048, F32)
        T1 = bview(SCR, 2048, F32)
        T2 = bview(SCR + 2048, 2048, F32)
        M1 = bview(SCR + 4096, 2048, F32)
        M2 = bview(SCR + 6144, 2048, F32)
        X1 = bview(0, 65536, F32).rearrange("p (c w) -> p c w", w=1024)
        WO = bview(65536, 16384).rearrange("p (k n) -> p k n", k=8)
        h2T = bview(65536, 32768).rearrange("p (k t) -> p k t", k=8)
        AQ = bview(102400, 16384).rearrange("p (f t) -> p f t", f=FQ)
        W2S = [bview(118784 + i * 8192, 8192).rearrange("p (f n) -> p f n", f=FQ) for i in range(2)]
        RR = [bview(SCR + i * 2048, 2048, F32) for i in range(2)]

        XS = [bview(69632 + i * 4096, 4096, F32) for i in range(NTC)]
        WB = [sb(f"WB{i}", [128, 8, 512], BF16) for i in range(3)]
        F4K = [sb(f"F4K{i}", [128, 1024], F32) for i in range(2)]
        XN = [sb(f"XN{i}", [128, 1024], BF16) for i in range(2)]
        G1H = sb("G1H", [128, 1024], F32)
        G2 = sb("G2", [128, 1024], F32)
        FNWB = bview(65536, 4096, F32)
        ident = sb("ident", [128, 128], BF16)
        MASK4 = sb("MASK4", [128, 2, 4, 128], BF16)
        TRI = sb("TRI", [128, 3, 128], F32)
        cvF = sb("cvF", [128, 8, 3], F32)
        thF = sb("thF", [128, 8, 3], F32)
        S2 = sb("S2", [128, 8, 4], BF16)
        S2rep = bview(36864, 2048).rearrange("p (k m) -> p k m", k=8)
        bmodF = sb("bmodFs", [128, 48], F32)
        n1wF = sb("n1wFs", [128, 8], F32)
        n2wF = sb("n2wFs", [128, 8], F32)
        modF = sb("modF", [128, 48, 3], F32)
        A1 = sb("A1", [128, 8, 3], F32)
        A2 = sb("A2", [128, 8, 3], F32)
        bcvF = sb("bcvFs", [128, 24], F32)
        bmgF = sb("bmgFs", [128, 16], F32)
        cwF = sb("cwFs", [128, 8, 3], F32)
        WGS = sb("WGS", [128, 8, 32], BF16)
        BGB = sb("BGB", [128, 32], F32)
        BT = [sb(f"BT{i}", [128, 384], F32) for i in range(1)]
        MW = [sb(f"MW{i}", [128, 128], F32) for i in range(2)]
        G = sb("G", [128, NTC, 32], F32)
        NLF = sb("NLF", [128, NTC, 16], F32)
        gt = {}
        for d in range(2):
            for nm in ("NB", "NBL", "EQ", "EK", "EKH", "DEC", "TMP"):
                gt[(nm, d)] = sb(f"{nm}{d}", [128, NTC * 8], F32)
        AD = sb("AD", [128, 16, 2], F32)
        RD = sb("RD", [128, 16, 2], F32)
        SSQ = sb("SSQ", [128, 16], F32)
        DUM = sb("DUM", [128, 2], F32)

        PA = [pst(f"pa{i}", [128, 512], F32) for i in range(2)]
        PT = [pst(f"pt{i}", [128, 8, 128], BF16) for i in range(2)]
        PS = [pst(f"ps{i}", [128, 4, 128], F32) for i in range(2)]
        PN = [pst(f"pn{i}", [128, 2, 256], F32) for i in range(2)]

        sems = {e: es.enter_context(nc.semaphore("s_" + e)) for e in ENGS}
        dsems = {e: [es.enter_context(nc.semaphore(f"d_{e}{i}")) for i in range(N_DMA_SEMS)] for e in ("sp", "pool", "act")}
        block = es.enter_context(nc.Block())

        rot = {}

        def nxt(name, n):
            v = rot.get(name, 0)
            rot[name] = v + 1
            return v % n

        def mm(out, lhsT, rhs, start, stop, r, w):
            S.op("pe", lambda e: e.matmul(out, lhsT=lhsT, rhs=rhs, start=start, stop=stop), r, w)

        def tr(out, in_, r, w):
            S.op("pe", lambda e: e.transpose(out=out, in_=in_, identity=ident[:]), list(r) + ["ident"], w)

        def act(out, in_, func, r, w, **kw):
            S.op("act", lambda e: e.activation(out=out, in_=in_, func=func, **kw), r, w)

        def tt(eng, out, in0, in1, op, r, w):
            S.op(eng, lambda e: e.tensor_tensor(out=out, in0=in0, in1=in1, op=op), r, w)

        def ts2(eng, out, in0, s1, s2, op0, op1, r, w):
            S.op(eng, lambda e: e.tensor_scalar(out=out, in0=in0, scalar1=s1, scalar2=s2, op0=op0, op1=op1), r, w)

        def tsmul(eng, out, in0, s1, r, w):
            S.op(eng, lambda e: e.tensor_scalar_mul(out=out, in0=in0, scalar1=s1), r, w)

        def tsadd(eng, out, in0, s1, r, w):
            S.op(eng, lambda e: e.tensor_scalar_add(out=out, in0=in0, scalar1=s1), r, w)

        def tsmax(eng, out, in0, s1, r, w):
            S.op(eng, lambda e: e.tensor_scalar_max(out=out, in0=in0, scalar1=s1), r, w)

        def stt(eng, out, in0, scalar, in1, op0, op1, r, w):
            S.op(eng, lambda e: e.scalar_tensor_tensor(out=out, in0=in0, scalar=scalar, in1=in1, op0=op0, op1=op1), r, w)

        def cp(eng, out, in_, r, w):
            if eng == "act":
                S.op("act", lambda e: e.copy(out=out, in_=in_), r, w)
            else:
                S.op(eng, lambda e: e.tensor_copy(out=out, in_=in_), r, w)

        def memset(eng, ap, val, w):
            S.op(eng, lambda e: e.memset(ap, val), [], w)

        def dma(q, out, in_, r, w):
            S.op(q, lambda e: e.dma_start(out=out, in_=in_), r, w, dma=True)

        def barrier():
            S.barrier("dve", lambda e: e.memset(DUM[:, 0:1], 0.0))

        PAX = [(PA[0][:], "pa0"), (PA[1][:], "pa1"),
               (PS[0][:].rearrange("p a t -> p (a t)"), "ps0"), (PS[1][:].rearrange("p a t -> p (a t)"), "ps1"),
               (PN[0][:].rearrange("p a t -> p (a t)"), "pn0"), (PN[1][:].rearrange("p a t -> p (a t)"), "pn1")]

        def pax():
            return PAX[nxt("pax", len(PAX))]

        def wslot():
            i = nxt("wb", 3)
            return WB[i], f"WB{i}"

        def wloop(n, src_fn, ncols):
            pend = load_w(src_fn(0), ncols)
            for j in range(n):
                cur = pend
                if j + 1 < n:
                    pend = load_w(src_fn(j + 1), ncols)
                yield j, cur

        def load_w(src_cols_ap, ncols):
            W, key = wslot()
            dma("pool", W[:, :, 0:ncols], src_cols_ap.rearrange("(k p) n -> p k n", p=128), [], [key])
            return W, key

        memset("pool", ident[:], 0.0, ["ident"])
        S.op("pool", lambda e: e.affine_select(out=ident[:], in_=ident[:], pattern=[[-1, 128]], compare_op=ALU.not_equal,
                                               fill=1.0, base=0, channel_multiplier=1), ["ident"], ["ident"])
        memset("pool", MASK4[:], 1.0, ["MASK4"])
        memset("pool", TRI[:], 1.0, ["TRI"])
        for dd in range(2):
            sg = 1 if dd == 0 else -1
            for sl in range(4):
                S.op("pool", lambda e, sl=sl, sg=sg, dd=dd: e.affine_select(
                    out=MASK4[:, dd, sl, :], in_=MASK4[:, dd, sl, :], pattern=[[sg, 128]], compare_op=ALU.is_ge,
                    fill=0.0, base=0, channel_multiplier=-sg), ["MASK4"], ["MASK4"])
        for sl in range(2):
            sg = 1 if sl == 0 else -1
            S.op("pool", lambda e, sl=sl, sg=sg: e.affine_select(
                out=TRI[:, sl, :], in_=TRI[:, sl, :], pattern=[[sg, 128]], compare_op=ALU.is_ge,
                fill=0.0, base=0, channel_multiplier=-sg), ["TRI"], ["TRI"])
        dma("sp", cvF[:], cvecF_d, [], ["cvF"])
        dma("sp", bmodF[:], bmodF_d, [], ["bmodF"])
        dma("sp", n1wF[:], n1wF_d, [], ["n1wF"])
        dma("sp", n2wF[:], n2wF_d, [], ["n2wF"])
        dma("sp", bcvF[:], bcvF_d, [], ["bcvF"])
        dma("sp", bmgF[:], bmgF_d, [], ["bmgF"])
        dma("sp", cwF[:], cwF_d, [], ["cwF"])
        dma("sp", BGB[:], bg_d.partition_broadcast(128), [], ["BGB"])
        dma("pool", WGS[:], wg_d.rearrange("(k p) n -> p k n", p=128), [], ["WGS"])
        tsmul("dve", bmgF[:], bmgF[:], 0.5, ["bmgF"], ["bmgF"])

        act(thF[:], cvF[:], AF.Tanh, ["cvF"], ["thF"], scale=0.5)
        memset("dve", S2[:], 0.0, ["S2"])
        stt("dve", S2[:, :, 0:3], thF[:], 1.0, cvF[:], ALU.add, ALU.mult, ["thF", "cvF", "S2"], ["S2"])

        for blk in (0, 1, 2, 3, 6, 7, 8, 9):
            W, wk = load_w(wmod_d[:, blk * 512:(blk + 1) * 512], 512)
            pi = nxt("pa", 2)
            pa = PA[pi]
            for jj in range(4):
                for k in range(8):
                    mm(pa[:, jj * 4:jj * 4 + 3], W[:, k, jj * 128:(jj + 1) * 128], S2[:, k, 0:3], k == 0, k == 7,
                       [wk, "S2"], [f"pa{pi}"])
            j0 = blk * 4
            stt("dve", modF[:, j0:j0 + 4, :], pa[:, 0:16].rearrange("p (a b) -> p a b", b=4)[:, :, 0:3], 0.5,
                bmodF[:, j0:j0 + 4].unsqueeze(2).to_broadcast([128, 4, 3]), ALU.mult, ALU.add,
                [f"pa{pi}", "bmodF"], ["modF"])
        stt("dve", A1[:], modF[:, 8:16, :], 1.0, n1wF[:].unsqueeze(2).to_broadcast([128, 8, 3]), ALU.add, ALU.mult,
            ["modF", "n1wF"], ["A1"])
        stt("dve", A2[:], modF[:, 32:40, :], 1.0, n2wF[:].unsqueeze(2).to_broadcast([128, 8, 3]), ALU.add, ALU.mult,
            ["modF", "n2wF"], ["A2"])

        SSB = sb("SSB", [128, 2, NTC], F32)

        def sumsq_tile(src, src_keys, i):
            if i % 2 == 0:
                act(XN[0][:], src, AF.Square, list(src_keys), ["XN0", f"SSB{i}"], accum_out=SSB[:, 0, i:i + 1])
            else:
                S.op("dve", lambda e: e.scalar_tensor_tensor(out=XN[1][:], in0=src, scalar=1.0, in1=src, op0=ALU.mult, op1=ALU.mult,
                                                             accum_out=SSB[:, 0, i:i + 1]), list(src_keys), ["XN1", f"SSB{i}"])

        def rstd_all(n):
            ks = [f"SSB{i}" for i in range(n)]
            ts2("dve", SSB[:, 1, 0:n], SSB[:, 0, 0:n], 1.0 / D, EPS, ALU.mult, ALU.add, ks, ["RSB"])
            act(SSB[:, 1, 0:n], SSB[:, 1, 0:n], AF.Ln, ["RSB"], ["RSB"])
            act(SSB[:, 1, 0:n], SSB[:, 1, 0:n], AF.Exp, ["RSB"], ["RSB"], scale=-0.5)

        def norm_apply_T(src, src_keys, i, dst, dst_key, tok0, Avec, SHvec, vec_keys):
            xn = XN[0]
            act(xn[:], src, AF.Copy, list(src_keys) + ["RSB"], ["XN0"], scale=SSB[:, 1, i:i + 1])
            pi = nxt("pt", 2)
            pt = PT[pi]
            for k in range(8):
                tr(pt[:, k, :], xn[:, k * 128:(k + 1) * 128], ["XN0"], [f"pt{pi}"])
            for k in range(8):
                if k % 4 == 3:
                    act(dst[:, k, tok0:tok0 + 128], pt[:, k, :], AF.Identity, [f"pt{pi}"] + list(vec_keys), [dst_key],
                        scale=Avec[:, k:k + 1], bias=SHvec[:, k:k + 1])
                else:
                    ts2("dve", dst[:, k, tok0:tok0 + 128], pt[:, k, :], Avec[:, k:k + 1], SHvec[:, k:k + 1], ALU.mult, ALU.add,
                        [f"pt{pi}"] + list(vec_keys), [dst_key])

        def stage(n):
            if n > kstop:
                raise _Stop()

        try:
          for b in range(NB):
              barrier()
              cp("dve", S2rep, S2[:, :, b:b + 1].to_broadcast([128, 8, 128]), ["S2"], ["S2rep"])
              for (blk0, dst, dkey, c_ps, c_b) in ((4, G1H, "G1H", 0.25, 0.5), (10, G2, "G2", 0.5, 1.0)):
                  fi = nxt("f4k", 2)
                  bb = F4K[fi]
                  dma("sp", bb[:], bmod_d[blk0 * 512:blk0 * 512 + 1024].partition_broadcast(128), [], [f"F4K{fi}"])
                  for hb in range(2):
                      W, wk = load_w(wmod_d[:, (blk0 + hb) * 512:(blk0 + hb + 1) * 512], 512)
                      pi = nxt("pa", 2)
                      pa = PA[pi]
                      for k in range(8):
                          mm(pa[:], S2rep[:, k, :], W[:, k, :], k == 0, k == 7, [wk, "S2rep"], [f"pa{pi}"])
                      tsmul("dve", dst[:, hb * 512:(hb + 1) * 512], pa[:], c_ps, [f"pa{pi}"], [dkey])
                      stt("dve", dst[:, hb * 512:(hb + 1) * 512], bb[:, hb * 512:(hb + 1) * 512], c_b,
                          dst[:, hb * 512:(hb + 1) * 512], ALU.mult, ALU.add, [f"F4K{fi}", dkey], [dkey])

              stage(1)
              def xsrc(i):
                  return x_d[b, i * 128:(i + 1) * 128, :] if i < NT else ctx_d[b, (i - NT) * 128:(i - NT + 1) * 128, :]
              for i in range(NTC):
                  dma("sp", XS[i], xsrc(i), [], [f"XS{i}"])
              for i in range(NTC):
                  sumsq_tile(XS[i], [f"XS{i}"], i)
              rstd_all(NTC)
              for i in range(NTC):
                  r = b if i < NT else 2
                  norm_apply_T(XS[i], [f"XS{i}"], i, hT, "hT", i * 128, A1[:, :, r], modF[:, 0:8, r], ["A1", "modF"])
              stage(2)
              for i in range(NTC):
                  pi = 0 if i < NT else 1
                  sl = i % NT
                  for k in range(8):
                      mm(PA[pi][:, sl * 32:(sl + 1) * 32], hT[:, k, i * 128:(i + 1) * 128], WGS[:, k, :], k == 0, k == 7,
                         ["hT", "WGS"], [f"pa{pi}"])
              tt("dve", G[:, 0:NT, :], PA[0][:].rearrange("p (c w) -> p c w", w=32), BGB[:].unsqueeze(1).to_broadcast([128, NT, 32]),
                 ALU.add, ["pa0", "BGB"], ["G"])
              tt("dve", G[:, NT:NTC, :], PA[1][:, 0:64].rearrange("p (c w) -> p c w", w=32),
                 BGB[:].unsqueeze(1).to_broadcast([128, 2, 32]), ALU.add, ["pa1", "BGB"], ["G"])
              rot["pa"] = 0
              act(NLF[:], G[:, :, 16:32], AF.Exp, ["G"], ["NLF"], scale=-1.0)
              tsadd("dve", NLF[:], NLF[:], 1.0, ["NLF"], ["NLF"])
              act(NLF[:], NLF[:], AF.Ln, ["NLF"], ["NLF"])
              for d in range(2):
                  def g3(nm):
                      return gt[(nm, d)][:].rearrange("p (c h) -> p c h", h=8)
                  cp("dve", g3("TMP"), NLF[:, :, d * 8:(d + 1) * 8], ["NLF"], [f"TMP{d}"])
                  mm(PA[0][:, 0:144], TRI[:, d, :], gt[("TMP", d)][:], True, True, [f"TMP{d}", "TRI"], ["pa0"])
                  mm(PA[1][:, 0:144], TRI[:, 2, :], gt[("TMP", d)][:], True, True, [f"TMP{d}", "TRI"], ["pa1"])
                  cp("act", gt[("NB", d)][:], PA[0][:, 0:144], ["pa0"], [f"NB{d}"])
                  cp("act", gt[("NBL", d)][:], PA[1][:, 0:144], ["pa1"], [f"NBL{d}"])
                  act(gt[("EQ", d)][:], gt[("NB", d)][:], AF.Exp, [f"NB{d}"], [f"EQ{d}"], scale=-1.0)
                  tt("dve", g3("TMP"), G[:, :, d * 8:(d + 1) * 8], g3("NB"), ALU.add, ["G", f"NB{d}"], [f"TMP{d}"])
                  act(gt[("EK", d)][:], gt[("TMP", d)][:], AF.Exp, [f"TMP{d}"], [f"EK{d}"])
                  tsmul("dve", gt[("EK", d)][:], gt[("EK", d)][:], 0.125, [f"EK{d}"], [f"EK{d}"])
                  tt("dve", gt[("TMP", d)][:], gt[("TMP", d)][:], gt[("NBL", d)][:], ALU.subtract, [f"TMP{d}", f"NBL{d}"], [f"TMP{d}"])
                  act(gt[("EKH", d)][:], gt[("TMP", d)][:], AF.Exp, [f"TMP{d}"], [f"EKH{d}"])
                  tsmul("dve", gt[("EKH", d)][:], gt[("EKH", d)][:], 0.125, [f"EKH{d}"], [f"EKH{d}"])
                  act(gt[("DEC", d)][:], gt[("NBL", d)][:], AF.Exp, [f"NBL{d}"], [f"DEC{d}"], scale=-1.0)

              stage(3)
              orders = ([16, 17] + list(range(16)), [17, 16] + list(range(15, -1, -1)))
              tmk = [f"TM{i}" for i in range(NTC)]
              S.op("pool", lambda e: e.memset(TM[:, :, 384:385], 1.0), ["hT"], ["TM1"])
              pending = []

              def build_post(h, mi):
                  P = []
                  P.append(lambda: stt("dve", AD[:], HN[:, :, :, 128], -1.0, HN[:, :, :, 128], ALU.mult, ALU.max, ["HN"], ["AD"]))
                  P.append(lambda: tsmax("dve", AD[:], AD[:], 1.0, ["AD"], ["AD"]))
                  P.append(lambda: S.op("dve", lambda e: e.reciprocal(out=RD[:], in_=AD[:]), ["AD"], ["RD"]))
                  P.append(lambda: tt("dve", HS[:], HN[:, :, 0, 0:128], RD[:, :, 0:1].to_broadcast([128, NT, 128]), ALU.mult,
                                      ["HN", "RD"], ["HS", "KQT"]))
                  P.append(lambda: tt("dve", SQ[:], HN[:, :, 1, 0:128], RD[:, :, 1:2].to_broadcast([128, NT, 128]), ALU.mult,
                                      ["HN", "RD"], ["SQ", "CSf0", "CSf1"]))
                  P.append(lambda: tt("dve", HS[:], HS[:], SQ[:], ALU.add, ["HS", "SQ"], ["HS", "KQT"]))
                  P.append(lambda: tt("dve", SQ[:], HS[:], HS[:], ALU.mult, ["HS"], ["SQ", "CSf0", "CSf1"]))
                  P.append(lambda: S.op("dve", lambda e: e.tensor_reduce(out=SSQ[:], in_=SQ[:], axis=AX.X, op=ALU.add), ["SQ"], ["SSQ"]))
                  P.append(lambda: ts2("dve", SSQ[:], SSQ[:], 1.0 / 128, EPS, ALU.mult, ALU.add, ["SSQ"], ["SSQ"]))
                  P.append(lambda: act(SSQ[:], SSQ[:], AF.Ln, ["SSQ"], ["SSQ"]))
                  P.append(lambda: act(SSQ[:], SSQ[:], AF.Exp, ["SSQ"], ["SSQ"], scale=-0.5))
                  P.append(lambda: tt("dve", HS[:], HS[:], SSQ[:].unsqueeze(2).to_broadcast([128, NT, 128]), ALU.mult,
                                      ["HS", "SSQ"], ["HS", "KQT"]))
                  P.append(lambda: tt("dve", HS[:], HS[:], MW[mi][:].unsqueeze(1).to_broadcast([128, NT, 128]), ALU.mult,
                                      ["HS", f"MW{mi}"], ["HS", "KQT"]))
                  P.append(lambda: stt("dve", GT[:], TO[:], 1.0, HS[:], ALU.add, ALU.mult, ["TO", "HS"], ["GT", "CSf0", "CSf1", "CS"]))

                  def trs(i0):
                      pi = nxt("pt", 2)
                      for ii in range(8):
                          tr(PT[pi][:, ii, :], GT[:, i0 + ii, :], ["GT"], [f"pt{pi}"])
                      cp("act", hsT[:, h, i0 * 128:(i0 + 8) * 128].rearrange("p (a t) -> p a t", t=128), PT[pi][:],
                         [f"pt{pi}"], ["hsT"])
                  P.append(lambda: trs(0))
                  P.append(lambda: trs(8))
                  return P

              for h, (W, wk) in wloop(H, lambda hh: wtm_d[:, hh * 384:(hh + 1) * 384], 384):
                  if h >= 1:
                      stage(4)
                  bi = nxt("bt", 1)
                  dma("sp", BT[bi][:], btm_d[h * 384:(h + 1) * 384].partition_broadcast(128), [], [f"BT{bi}"])
                  mi = nxt("mw", 2)
                  dma("sp", MW[mi][:], mnw_d[h * 128:(h + 1) * 128].partition_broadcast(128), [], [f"MW{mi}"])
                  tsmul("dve", MW[mi][:], MW[mi][:], 0.5, [f"MW{mi}"], [f"MW{mi}"])
                  for i in range(NTC):
                      pa_ap, pa_k = PAX[nxt("pax4", 4)]
                      for k in range(8):
                          mm(pa_ap[:, 0:384], hT[:, k, i * 128:(i + 1) * 128], W[:, k, 0:384], k == 0, k == 7,
                             ["hT", wk], [pa_k])
                      tt("dve", TM[:, i, 0:384], pa_ap[:, 0:384], BT[bi][:], ALU.add, [pa_k, f"BT{bi}"], [f"TM{i}"])
                      if pending:
                          pending.pop(0)()
                  while pending:
                      pending.pop(0)()
                  stage(3.1)
                  def g3d(nm, d):
                      return gt[(nm, d)][:].rearrange("p (c h) -> p c h", h=8)
                  for d in range(2):
                      se = "pool" if d == 0 else "dve"
                      tt(se, KH[:, :, d, :], TM[:, :, 0:64], g3d("EKH", d)[:, :, h:h + 1].to_broadcast([128, NTC, 64]), ALU.mult,
                         tmk + [f"EKH{d}"], [f"KH{d}", "HN"])
                  for d in range(2):
                      se = "pool" if d == 0 else "dve"
                      tt(se, SCL[:, :, d, :], TM[:, :, 0:64], g3d("EK", d)[:, :, h:h + 1].to_broadcast([128, NTC, 64]), ALU.mult,
                         tmk + [f"EK{d}"], [f"SCL{d}", "HN"])
                  for d in range(2):
                      tt("dve", SCL[:, 0:NT, 2 + d, :], TM[:, 0:NT, 64:128], g3d("EQ", d)[:, 0:NT, h:h + 1].to_broadcast([128, NT, 64]),
                         ALU.mult, tmk + [f"EQ{d}"], [f"SCLQ{d}", "HN"])
                  stage(3.3)
                  for i0 in range(0, NTC, 2):
                      pi = nxt("pn", 2)
                      for ii in range(2):
                          i = i0 + ii
                          mm(PN[pi][:, ii, 0:129], KH[:, i].rearrange("p a b -> p (a b)"), TM[:, i, 256:385], True, True,
                             ["KH0", "KH1", f"TM{i}", "TM1"], [f"pn{pi}"])
                      cp("act", DC[:, i0:i0 + 2, :], PN[pi][:, :, 0:129], [f"pn{pi}"], ["DC", "HN", "TO"])
                  stage(3.4)
                  decs = [gt[("DEC", d)][:].rearrange("p (c h) -> p c h", h=8) for d in range(2)]
                  for d in range(2):
                      rows = slice(d * 64, (d + 1) * 64)
                      memset("dve", CSf[rows, orders[d][0], :], 0.0, [f"CSf{d}", "SQ", "GT"])
                  for kk in range(1, NTC):
                      for d in range(2):
                          rows = slice(d * 64, (d + 1) * 64)
                          c_prev, c = orders[d][kk - 1], orders[d][kk]
                          stt("dve", CSf[rows, c, :], CSf[rows, c_prev, :], decs[d][rows, c_prev, h:h + 1], DC[rows, c_prev, :],
                              ALU.mult, ALU.add, [f"CSf{d}", f"DEC{d}", "DC"], [f"CSf{d}"])
                  stage(3.2)
                  for i0 in range(0, NT, 4):
                      pi = nxt("pt", 2)
                      for ii in range(4):
                          i = i0 + ii
                          tr(PT[pi][:, 2 * ii, :], SCL[:, i, 0:2, :].rearrange("p a b -> p (a b)"), ["SCL0", "SCL1"], [f"pt{pi}"])
                          tr(PT[pi][:, 2 * ii + 1, :], SCL[:, i, 2:4, :].rearrange("p a b -> p (a b)"), ["SCLQ0", "SCLQ1"], [f"pt{pi}"])
                      cp("act", KQT[:, i0:i0 + 4].rearrange("p c a t -> p (c a) t"), PT[pi][:], [f"pt{pi}"], ["KQT", "HS"])
                  stage(3.45)
                  cp("act", CS[:, :, 0:129], CSf[:, 0:NT, :], ["CSf0", "CSf1"], ["CS", "GT"])
                  act(TO[:], TM[:, 0:NT, 128:256], AF.Tanh, tmk, ["TO", "DC"], scale=0.5)
                  stage(3.5)
                  def scores(i0):
                      for ii in range(4):
                          i = i0 + ii
                          for d in range(2):
                              rows = slice(d * 64, (d + 1) * 64)
                              mm(PS[d][:, ii, :], KQT[rows, i, 0, :], KQT[rows, i, 1, :], True, True, ["KQT"], [f"ps{d}"])
                      pq = nxt("pp", 2)
                      for d in range(2):
                          tt("dve", PP[pq * 2 + d][:], PS[d][:], MASK4[:, d], ALU.mult, [f"ps{d}", "MASK4"], [f"PP{pq * 2 + d}"])
                      return pq

                  def outputs(i0, pq):
                      for ii in range(4):
                          i = i0 + ii
                          pi = nxt("pn", 2)
                          for d in range(2):
                              rows = slice(d * 64, (d + 1) * 64)
                              mm(PN[pi][:, d, 0:129], PP[pq * 2 + d][:, ii, :], TM[:, i, 256:385], True, False,
                                 [f"PP{pq * 2 + d}", f"TM{i}", "TM1"], [f"pn{pi}"])
                              mm(PN[pi][:, d, 0:129], KQT[rows, i, 1, :], CS[rows, i, 0:129], False, True,
                                 ["KQT", "CS"], [f"pn{pi}"])
                          cp("act", HN[:, i, :, :], PN[pi][:, :, 0:129], [f"pn{pi}"],
                             ["HN", "SCL0", "SCL1", "SCLQ0", "SCLQ1", "KH0", "KH1", "DC"])

                  pq_prev = scores(0)
                  for i0 in range(0, NT, 4):
                      pq_next = scores(i0 + 4) if i0 + 4 < NT else None
                      outputs(i0, pq_prev)
                      pq_prev = pq_next
                  stage(3.6)
                  pending = build_post(h, mi)
              while pending:
                  pending.pop(0)()

              stage(5)
              barrier()
              for c, (W, wk) in wloop(8, lambda cc: wcv_d[:, cc * 384:(cc + 1) * 384], 384):
                  for g in range(4):
                      tsl = slice(g * 512, (g + 1) * 512)
                      for part in range(3):
                          pa_ap, pa_k = pax()
                          for k in range(8):
                              mm(pa_ap, W[:, k, part * 128:(part + 1) * 128], hT[:, k, tsl], k == 0, k == 7, ["hT", wk], [pa_k])
                          act(X3[part], pa_ap, AF.Identity, [pa_k, "bcvF"], [f"X3{part}"],
                              bias=bcvF[:, c * 3 + part:c * 3 + part + 1])
                      tt("pool", CU, X3[1], X3[0], ALU.mult, ["X31", "X30"], ["CU"])
                      tsmul("dve", CA, CU, cwF[:, c, 1:2], ["CU", "cwF"], ["CA"])
                      U3 = CU.rearrange("p (r w) -> p r w", w=64)
                      A3 = CA.rearrange("p (r w) -> p r w", w=64)
                      stt("dve", A3[:, :, 1:64], U3[:, :, 0:63], cwF[:, c, 0:1], A3[:, :, 1:64], ALU.mult, ALU.add, ["CU", "CA", "cwF"], ["CA"])
                      stt("dve", A3[:, :, 0:63], U3[:, :, 1:64], cwF[:, c, 2:3], A3[:, :, 0:63], ALU.mult, ALU.add, ["CU", "CA", "cwF"], ["CA"])
                      tt("pool", aT[:, c, tsl], CA, X3[2], ALU.mult, ["CA", "X32"], ["aT"])

              stage(6)
              barrier()
              for j, (W, wk) in wloop(8, lambda jj: wpost_d[:, jj * 512:(jj + 1) * 512], 512):
                  for g in range(4):
                      tsl = slice(g * 512, (g + 1) * 512)
                      for (which, src, skey, Tt, Mm) in ((0, aT, "aT", T1, M1), (1, hsT, "hsT", T2, M2)):
                          pa_ap, pa_k = pax()
                          for k in range(8):
                              mm(pa_ap, W[:, k, 256 + which * 128:384 + which * 128], hT[:, k, tsl], k == 0, k == 7, ["hT", wk], [pa_k])
                          act(Tt, pa_ap, AF.Tanh, [pa_k, "bmgF"], [f"T{which}"], scale=0.5,
                              bias=bmgF[:, j * 2 + which:j * 2 + which + 1])
                          pa_ap, pa_k = pax()
                          for k in range(8):
                              mm(pa_ap, W[:, k, which * 128:(which + 1) * 128], src[:, k, tsl], k == 0, k == 7, [skey, wk], [pa_k])
                          stt("dve", Mm, Tt, 1.0, pa_ap, ALU.add, ALU.mult, [f"T{which}", pa_k], [f"M{which}"])
                      tt("pool", mT[:, j, tsl], M1, M2, ALU.add, ["M0", "M1"], ["mT"])

              stage(7)
              barrier()
              for k in range(8):
                  fi = nxt("f4k", 2)
                  dma("sp", F4K[fi][:], wout_d[k * 128:(k + 1) * 128, :], [], [f"F4K{fi}"])
                  tt("dve", WO[:, k, :], F4K[fi][:], G1H[:], ALU.mult, [f"F4K{fi}", "G1H"], ["WO"])
              for i in range(NT):
                  dma("sp", X1[:, i, :], x_d[b, i * 128:(i + 1) * 128, :], [], [f"X1_{i}"])
                  for nb in range(2):
                      pa_ap, pa_k = pax()
                      for k in range(8):
                          mm(pa_ap, mT[:, k, i * 128:(i + 1) * 128], WO[:, k, nb * 512:(nb + 1) * 512], k == 0, k == 7, ["mT", "WO"], [pa_k])
                      tt("dve", X1[:, i, nb * 512:(nb + 1) * 512], pa_ap, X1[:, i, nb * 512:(nb + 1) * 512], ALU.add,
                         [pa_k, f"X1_{i}"], [f"X1_{i}"])
                  sumsq_tile(X1[:, i, :], [f"X1_{i}"], i)
              rstd_all(NT)
              barrier()
              for i in range(NT):
                  norm_apply_T(X1[:, i, :], [f"X1_{i}"], i, h2T, "h2T", i * 128, A2[:, :, b], modF[:, 24:32, b], ["A2", "modF"])

              stage(8)
              for qd, (W1, w1k) in wloop(NQ, lambda qq: wff1_d[:, qq * 512:(qq + 1) * 512], 512):
                  par = qd % 2
                  for fl in range(FQ):
                      fi = nxt("f4k", 2)
                      f = qd * FQ + fl
                      dma("sp", F4K[fi][:], wff2_d[f * 128:(f + 1) * 128, :], [], [f"F4K{fi}"])
                      tt("pool", W2S[par][:, fl, :], F4K[fi][:], G2[:], ALU.mult, [f"F4K{fi}", "G2"], [f"W2S{par}"])
                  for fl in range(FQ):
                      for g in range(4):
                          tsl = slice(g * 512, (g + 1) * 512)
                          pa_ap, pa_k = pax()
                          for k in range(8):
                              mm(pa_ap, W1[:, k, fl * 128:(fl + 1) * 128], h2T[:, k, tsl], k == 0, k == 7, ["h2T", w1k], [pa_k])
                          ri = nxt("rr", 2)
                          act(RR[ri], pa_ap, AF.Relu, [pa_k], [f"RR{ri}"])
                          tt("pool", AQ[:, fl, tsl], RR[ri], RR[ri], ALU.mult, [f"RR{ri}"], ["AQ"])
                  for i in range(NT):
                      for nb in range(2):
                          pa_ap, pa_k = pax()
                          for fl in range(FQ):
                              mm(pa_ap, AQ[:, fl, i * 128:(i + 1) * 128], W2S[par][:, fl, nb * 512:(nb + 1) * 512], fl == 0, fl == FQ - 1,
                                 ["AQ", f"W2S{par}"], [pa_k])
                          tt("dve", X1[:, i, nb * 512:(nb + 1) * 512], pa_ap, X1[:, i, nb * 512:(nb + 1) * 512], ALU.add,
                             [pa_k, f"X1_{i}"], [f"X1_{i}"])
                      if qd == NQ - 1:
                          sumsq_tile(X1[:, i, :], [f"X1_{i}"], i)

              stage(9)
              barrier()
              dma("sp", FNWB, fnw_d.partition_broadcast(128), [], ["FNWB"])
              rstd_all(NT)
              for i in range(NT):
                  fi = nxt("f4k", 2)
                  stt("dve", F4K[fi][:], X1[:, i, :], SSB[:, 1, i:i + 1], FNWB, ALU.mult, ALU.mult, [f"X1_{i}", "RSB", "FNWB"], [f"F4K{fi}"])
                  dma("sp", out_d[b, i * 128:(i + 1) * 128, :], F4K[fi][:], [f"F4K{fi}"], [])
        except _Stop:
            pass

        S.emit({"pe": block.tensor, "act": block.scalar, "dve": block.vector, "pool": block.gpsimd, "sp": block.sync}, sems, dsems)
    return nc


def _fm(v, nchunk):
    return np.ascontiguousarray(np.asarray(v, np.float32).reshape(nchunk, 128).T)


def kernel(x, c, ctx, c_ctx, w_mod, b_mod, norm1_w, w_in, b_in, conv_w, mlstm_norm_w,
           w_conv_out, w_mlstm_out, w_out, norm2_w, w_ff1, w_ff2, final_norm_w):
    f32 = np.float32
    x = np.asarray(x, f32)
    c = np.asarray(c, f32)
    ctx = np.asarray(ctx, f32)
    c_ctx = np.asarray(c_ctx, f32)
    w_in0 = np.asarray(w_in, f32)[0]
    b_in0 = np.asarray(b_in, f32)[0]
    B = x.shape[0]
    NB = B // N_CORES

    o_k, o_v, o_ig, o_fg, o_q, o_o = 0, 512, 1536, 1552, 1568, 2080
    o_xin, o_gc, o_gb, o_mg = 3104, 4128, 5152, 6176
    cols_tm, cols_cv, cols_mg = [], [], []
    for h in range(H):
        cols_tm += list(range(o_k + h * 64, o_k + (h + 1) * 64))
        cols_tm += list(range(o_q + h * 64, o_q + (h + 1) * 64))
        cols_tm += list(range(o_o + h * 128, o_o + (h + 1) * 128))
        cols_tm += list(range(o_v + h * 128, o_v + (h + 1) * 128))
    for ch in range(8):
        for base in (o_xin, o_gc, o_gb):
            cols_cv += list(range(base + ch * 128, base + (ch + 1) * 128))
    cols_g = list(range(o_ig, o_ig + 16)) + list(range(o_fg, o_fg + 16))
    wco = np.asarray(w_conv_out, f32)[0]
    wmo = np.asarray(w_mlstm_out, f32)[0]
    wpost = np.concatenate(
        [np.concatenate([wco[:, j * 128:(j + 1) * 128], wmo[:, j * 128:(j + 1) * 128],
                         w_in0[:, o_mg + j * 128:o_mg + (j + 1) * 128],
                         w_in0[:, o_mg + 1024 + j * 128:o_mg + 1024 + (j + 1) * 128]], axis=1) for j in range(8)], axis=1)
    bcvF = np.stack([b_in0[base + ch * 128:base + (ch + 1) * 128] for ch in range(8) for base in (o_xin, o_gc, o_gb)], axis=1)
    bmgF = np.stack([b_in0[o_mg + w * 1024 + j * 128:o_mg + w * 1024 + (j + 1) * 128] for j in range(8) for w in range(2)], axis=1)
    shared = {
        "wmod": np.ascontiguousarray(np.asarray(w_mod, f32)[0]),
        "bmodF": _fm(np.asarray(b_mod, f32)[0], 48),
        "bmod": np.ascontiguousarray(np.asarray(b_mod, f32)[0]),
        "n1wF": _fm(np.asarray(norm1_w, f32)[0], 8),
        "n2wF": _fm(np.asarray(norm2_w, f32)[0], 8),
        "wg": np.ascontiguousarray(w_in0[:, cols_g]),
        "wtm": np.ascontiguousarray(w_in0[:, cols_tm]),
        "wcv": np.ascontiguousarray(w_in0[:, cols_cv]),
        "wpost": np.ascontiguousarray(wpost),
        "bg": np.ascontiguousarray(b_in0[cols_g]),
        "btm": np.ascontiguousarray(b_in0[cols_tm]),
        "bcvF": np.ascontiguousarray(bcvF),
        "bmgF": np.ascontiguousarray(bmgF),
        "cwF": np.ascontiguousarray(np.asarray(conv_w, f32)[0].reshape(3, 8, 128).transpose(2, 1, 0)),
        "mnw": np.ascontiguousarray(np.asarray(mlstm_norm_w, f32)[0]),
        "wout": np.ascontiguousarray(np.asarray(w_out, f32)[0]),
        "wff1": np.ascontiguousarray(np.asarray(w_ff1, f32)[0]),
        "wff2": np.ascontiguousarray(np.asarray(w_ff2, f32)[0]),
        "fnw": np.ascontiguousarray(np.asarray(final_norm_w, f32)),
    }
    in_maps = []
    for core in range(N_CORES):
        bs = slice(core * NB, (core + 1) * NB)
        cv = np.concatenate([c[bs], c_ctx[None, :]], axis=0)
        assert NB == 2
        m = dict(shared)
        m["x"] = np.ascontiguousarray(x[bs])
        m["ctx"] = np.ascontiguousarray(ctx[bs])
        m["cvecF"] = np.ascontiguousarray(cv.reshape(3, 8, 128).transpose(2, 1, 0))
        in_maps.append(m)
    nc = build_program(NB)
    res = run_bass_kernel_spmd(nc, in_maps, core_ids=list(range(N_CORES)))
    out = np.concatenate([np.asarray(r["out"], f32) for r in res.results], axis=0)
    return out
```

```python
import math
from contextlib import ExitStack

import numpy as np
import concourse.bass as bass
import concourse.mybir as mybir
from concourse.bass_utils import run_bass_kernel_spmd

F32 = mybir.dt.float32
BF16 = mybir.dt.bfloat16
AF = mybir.ActivationFunctionType
ALU = mybir.AluOpType
AX = mybir.AxisListType

D = 1024
T = 2048
CT = 256
NT = 16
NTC = 18
H = 8
EPS = 1e-6
N_CORES = 8
NQ = 8
FQ = 32 // NQ

ENGS = ("pe", "act", "dve", "pool", "sp")
N_DMA_SEMS = 8
import os
KVAR = int(os.environ.get('KVAR', '0'))


class Op:
    __slots__ = ("eng", "fn", "dma", "deps", "signal", "sig_sem", "sig_val", "idx")

    def __init__(self, eng, fn, dma):
        self.eng = eng
        self.fn = fn
        self.dma = dma
        self.deps = set()
        self.signal = False
        self.sig_sem = None
        self.sig_val = 0


class Sched:
    def __init__(self):
        self.ops = []
        self.eops = {e: [] for e in ENGS}
        self.last_w = {}
        self.readers = {}
        self.epoch_op = None
        self.dma_since = []

    def barrier(self, eng, fn):
        o = Op(eng, fn, False)
        o.idx = len(self.ops)
        for e in ENGS:
            for p in reversed(self.eops[e]):
                if not p.dma:
                    o.deps.add(p)
                    break
        for p in self.dma_since:
            o.deps.add(p)
        if self.epoch_op is not None:
            o.deps.add(self.epoch_op)
        self.dma_since = []
        self.epoch_op = o
        self.ops.append(o)
        self.eops[eng].append(o)
        return o

    def op(self, eng, fn, reads=(), writes=(), dma=False, epoch=True):
        o = Op(eng, fn, dma)
        o.idx = len(self.ops)
        reads = list(reads)
        if epoch and self.epoch_op is not None:
            o.deps.add(self.epoch_op)
        if dma:
            self.dma_since.append(o)
        for k in reads:
            w = self.last_w.get(k)
            if w is not None:
                o.deps.add(w)
        for k in writes:
            w = self.last_w.get(k)
            if w is not None:
                o.deps.add(w)
            for r in self.readers.get(k, {}).values():
                o.deps.add(r)
        for k in reads:
            self.readers.setdefault(k, {})[("dma", o.idx) if dma else eng] = o
        for k in writes:
            self.last_w[k] = o
            self.readers[k] = {}
        o.deps.discard(o)
        self.ops.append(o)
        self.eops[eng].append(o)
        return o

    def emit(self, block_engines, sems, dma_sems):
        for o in self.ops:
            nd = set()
            for d in o.deps:
                if (not d.dma) and (not o.dma) and d.eng == "pe" and o.eng == "pe":
                    continue
                nd.add(d)
                d.signal = True
            o.deps = nd
        cnt = {e: 0 for e in ENGS}
        dcnt = {e: 0 for e in ENGS}
        dsem_val = {}
        prev_on_sem = {}
        for e in ENGS:
            for o in self.eops[e]:
                if o.dma:
                    i = dcnt[e] % N_DMA_SEMS
                    dcnt[e] += 1
                    dsem_val[(e, i)] = dsem_val.get((e, i), 0) + 16
                    o.sig_sem = dma_sems[e][i]
                    o.sig_val = dsem_val[(e, i)]
                    p = prev_on_sem.get((e, i))
                    if p is not None:
                        o.deps.add(p)
                    prev_on_sem[(e, i)] = o
                    o.signal = True
                elif o.signal:
                    cnt[e] += 1
                    o.sig_sem = sems[e]
                    o.sig_val = cnt[e]
        finals = list(prev_on_sem.values())

        def run(eng_name):
            def body(eng):
                waited = {}
                for o in self.eops[eng_name]:
                    need = {}
                    for d in o.deps:
                        k = id(d.sig_sem)
                        if k not in need or need[k][1] < d.sig_val:
                            need[k] = (d.sig_sem, d.sig_val)
                    for k, (s, v) in need.items():
                        if waited.get(k, 0) >= v:
                            continue
                        eng.wait_ge(s, v)
                        waited[k] = v
                    ins = o.fn(eng)
                    if o.signal:
                        ins.then_inc(o.sig_sem, 16 if o.dma else 1)
                if eng_name == "sp":
                    for o in finals:
                        k = id(o.sig_sem)
                        if waited.get(k, 0) >= o.sig_val:
                            continue
                        eng.wait_ge(o.sig_sem, o.sig_val)
                        waited[k] = o.sig_val
            return body

        for e in ENGS:
            block_engines[e](run(e))


class _Stop(Exception):
    pass


def build_program(NB, kstop=99):
    nc = bass.Bass("TRN2", target_bir_lowering=False)

    def dram(name, shape, kind="ExternalInput"):
        return nc.dram_tensor(name, list(shape), F32, kind=kind).ap()

    x_d = dram("x", [NB, T, D])
    ctx_d = dram("ctx", [NB, CT, D])
    cvecF_d = dram("cvecF", [128, 8, 3])
    wmod_d = dram("wmod", [D, 6 * D])
    bmodF_d = dram("bmodF", [128, 48])
    bmod_d = dram("bmod", [6 * D])
    n1wF_d = dram("n1wF", [128, 8])
    n2wF_d = dram("n2wF", [128, 8])
    wg_d = dram("wg", [D, 32])
    wtm_d = dram("wtm", [D, 8 * 384])
    wcv_d = dram("wcv", [D, 8 * 384])
    wpost_d = dram("wpost", [D, 8 * 512])
    bg_d = dram("bg", [32])
    btm_d = dram("btm", [8 * 384])
    bcvF_d = dram("bcvF", [128, 24])
    bmgF_d = dram("bmgF", [128, 16])
    cwF_d = dram("cwF", [128, 8, 3])
    mnw_d = dram("mnw", [D])
    wout_d = dram("wout", [D, D])
    wff1_d = dram("wff1", [D, 4 * D])
    wff2_d = dram("wff2", [4 * D, D])
    fnw_d = dram("fnw", [D])
    out_d = dram("out", [NB, T, D], kind="ExternalOutput")

    S = Sched()
    with ExitStack() as es:
        def sb(name, shape, dt):
            return es.enter_context(nc.sbuf_tensor(name, list(shape), dt))

        def pst(name, shape, dt):
            return es.enter_context(nc.psum_tensor(name, list(shape), dt))

        BIGB = 143360
        BIG = sb("BIG", [128, BIGB // 2], BF16)

        def bview(off_bytes, nbytes, dt=BF16):
            v = BIG[:, off_bytes // 2:(off_bytes + nbytes) // 2]
            if dt == F32:
                v = v.bitcast(F32)
            return v

        hT = bview(0, 36864).rearrange("p (k t) -> p k t", k=8)
        hsT = bview(36864, 32768).rearrange("p (k t) -> p k t", k=8)
        aT = bview(69632, 32768).rearrange("p (k t) -> p k t", k=8)
        mT = bview(102400, 32768).rearrange("p (k t) -> p k t", k=8)
        SCR = 135168
        A0 = 69632
        TM = bview(A0, 18 * 386 * 2).rearrange("p (c w) -> p c w", w=386)
        o1 = A0 + 18 * 386 * 2
        SCL = bview(o1, 9216).rearrange("p (c a b) -> p c a b", a=4, b=64)
        KH = bview(o1 + 9216, 4608).rearrange("p (c a b) -> p c a b", a=2, b=64)
        DC = bview(o1 + 13824, 9288, F32).rearrange("p (c w) -> p c w", w=129)
        HN = bview(o1, 16512, F32).rearrange("p (c a w) -> p c a w", a=2, w=129)
        TO = bview(o1 + 16512, 4096).rearrange("p (c w) -> p c w", w=128)
        o2 = o1 + 23112
        KQT = bview(o2, 8192).rearrange("p (c a t) -> p c a t", a=2, t=128)
        CSf = bview(o2 + 8192, 9288, F32).rearrange("p (c w) -> p c w", w=129)
        CS = bview(o2 + 17480, 4160).rearrange("p (c w) -> p c w", w=130)
        HS = bview(o2, 8192, F32).rearrange("p (c w) -> p c w", w=128)
        SQ = bview(o2 + 8192, 8192, F32).rearrange("p (c w) -> p c w", w=128)
        GT = bview(o2 + 16384, 4096).rearrange("p (c w) -> p c w", w=128)
        o3 = o2 + 21640
        PP = [bview(o3 + i * 1024, 1024).rearrange("p (a t) -> p a t", a=4) for i in range(4)]
        assert o3 + 4096 <= 135168
        X3 = [bview(102400 + i * 2048, 2048, F32) for i in range(3)]
        CU = bview(102400 + 6144, 2048, F32)
        CA = bview(102400 + 8192, 2048, F32)
        T1 = bview(SCR, 2048, F32)
        T2 = bview(SCR + 2048, 2048, F32)
        M1 = bview(SCR + 4096, 2048, F32)
        M2 = bview(SCR + 6144, 2048, F32)
        X1 = bview(0, 65536, F32).rearrange("p (c w) -> p c w", w=1024)
        WO = bview(65536, 16384).rearrange("p (k n) -> p k n", k=8)
        h2T = bview(65536, 32768).rearrange("p (k t) -> p k t", k=8)
        AQ = bview(102400, 16384).rearrange("p (f t) -> p f t", f=FQ)
        W2S = [bview(118784 + i * 8192, 8192).rearrange("p (f n) -> p f n", f=FQ) for i in range(2)]
        RR = [bview(SCR + i * 2048, 2048, F32) for i in range(2)]

        XS = [bview(69632 + i * 4096, 4096, F32) for i in range(NTC)]
        WB = [sb(f"WB{i}", [128, 8, 512], BF16) for i in range(3)]
        F4K = [sb(f"F4K{i}", [128, 1024], F32) for i in range(2)]
        XN = [sb(f"XN{i}", [128, 1024], BF16) for i in range(2)]
        G1H = sb("G1H", [128, 1024], F32)
        G2 = sb("G2", [128, 1024], F32)
        FNWB = bview(65536, 4096, F32)
        ident = sb("ident", [128, 128], BF16)
        MASK4 = sb("MASK4", [128, 2, 4, 128], BF16)
        TRI = sb("TRI", [128, 3, 128], F32)
        cvF = sb("cvF", [128, 8, 3], F32)
        thF = sb("thF", [128, 8, 3], F32)
        S2 = sb("S2", [128, 8, 4], BF16)
        S2rep = bview(36864, 2048).rearrange("p (k m) -> p k m", k=8)
        bmodF = sb("bmodFs", [128, 48], F32)
        n1wF = sb("n1wFs", [128, 8], F32)
        n2wF = sb("n2wFs", [128, 8], F32)
        modF = sb("modF", [128, 48, 3], F32)
        A1 = sb("A1", [128, 8, 3], F32)
        A2 = sb("A2", [128, 8, 3], F32)
        bcvF = sb("bcvFs", [128, 24], F32)
        bmgF = sb("bmgFs", [128, 16], F32)
        cwF = sb("cwFs", [128, 8, 3], F32)
        WGS = sb("WGS", [128, 8, 32], BF16)
        BGB = sb("BGB", [128, 32], F32)
        BT = [sb(f"BT{i}", [128, 384], F32) for i in range(1)]
        MW = [sb(f"MW{i}", [128, 128], F32) for i in range(2)]
        G = sb("G", [128, NTC, 32], F32)
        NLF = sb("NLF", [128, NTC, 16], F32)
        gt = {}
        for d in range(2):
            for nm in ("NB", "NBL", "EQ", "EK", "EKH", "DEC", "TMP"):
                gt[(nm, d)] = sb(f"{nm}{d}", [128, NTC * 8], F32)
        AD = sb("AD", [128, 16, 2], F32)
        RD = sb("RD", [128, 16, 2], F32)
        SSQ = sb("SSQ", [128, 16], F32)
        DUM = sb("DUM", [128, 2], F32)

        PA = [pst(f"pa{i}", [128, 512], F32) for i in range(2)]
        PT = [pst(f"pt{i}", [128, 8, 128], BF16) for i in range(2)]
        PS = [pst(f"ps{i}", [128, 4, 128], F32) for i in range(2)]
        PN = [pst(f"pn{i}", [128, 2, 256], F32) for i in range(2)]

        sems = {e: es.enter_context(nc.semaphore("s_" + e)) for e in ENGS}
        dsems = {e: [es.enter_context(nc.semaphore(f"d_{e}{i}")) for i in range(N_DMA_SEMS)] for e in ("sp", "pool", "act")}
        block = es.enter_context(nc.Block())

        rot = {}

        def nxt(name, n):
            v = rot.get(name, 0)
            rot[name] = v + 1
            return v % n

        def mm(out, lhsT, rhs, start, stop, r, w):
            S.op("pe", lambda e: e.matmul(out, lhsT=lhsT, rhs=rhs, start=start, stop=stop), r, w)

        def tr(out, in_, r, w):
            S.op("pe", lambda e: e.transpose(out=out, in_=in_, identity=ident[:]), list(r) + ["ident"], w)

        def act(out, in_, func, r, w, **kw):
            S.op("act", lambda e: e.activation(out=out, in_=in_, func=func, **kw), r, w)

        def tt(eng, out, in0, in1, op, r, w):
            S.op(eng, lambda e: e.tensor_tensor(out=out, in0=in0, in1=in1, op=op), r, w)

        def ts2(eng, out, in0, s1, s2, op0, op1, r, w):
            S.op(eng, lambda e: e.tensor_scalar(out=out, in0=in0, scalar1=s1, scalar2=s2, op0=op0, op1=op1), r, w)

        def tsmul(eng, out, in0, s1, r, w):
            S.op(eng, lambda e: e.tensor_scalar_mul(out=out, in0=in0, scalar1=s1), r, w)

        def tsadd(eng, out, in0, s1, r, w):
            S.op(eng, lambda e: e.tensor_scalar_add(out=out, in0=in0, scalar1=s1), r, w)

        def tsmax(eng, out, in0, s1, r, w):
            S.op(eng, lambda e: e.tensor_scalar_max(out=out, in0=in0, scalar1=s1), r, w)

        def stt(eng, out, in0, scalar, in1, op0, op1, r, w):
            S.op(eng, lambda e: e.scalar_tensor_tensor(out=out, in0=in0, scalar=scalar, in1=in1, op0=op0, op1=op1), r, w)

        def cp(eng, out, in_, r, w):
            if eng == "act":
                S.op("act", lambda e: e.copy(out=out, in_=in_), r, w)
            else:
                S.op(eng, lambda e: e.tensor_copy(out=out, in_=in_), r, w)

        def memset(eng, ap, val, w):
            S.op(eng, lambda e: e.memset(ap, val), [], w)

        def dma(q, out, in_, r, w):
            S.op(q, lambda e: e.dma_start(out=out, in_=in_), r, w, dma=True)

        def barrier():
            S.barrier("dve", lambda e: e.memset(DUM[:, 0:1], 0.0))

        PAX = [(PA[0][:], "pa0"), (PA[1][:], "pa1"),
               (PS[0][:].rearrange("p a t -> p (a t)"), "ps0"), (PS[1][:].rearrange("p a t -> p (a t)"), "ps1"),
               (PN[0][:].rearrange("p a t -> p (a t)"), "pn0"), (PN[1][:].rearrange("p a t -> p (a t)"), "pn1")]

        def pax():
            return PAX[nxt("pax", len(PAX))]

        def wslot():
            i = nxt("wb", 3)
            return WB[i], f"WB{i}"

        def wloop(n, src_fn, ncols):
            pend = load_w(src_fn(0), ncols)
            for j in range(n):
                cur = pend
                if j + 1 < n:
                    pend = load_w(src_fn(j + 1), ncols)
                yield j, cur

        def load_w(src_cols_ap, ncols):
            W, key = wslot()
            dma("pool", W[:, :, 0:ncols], src_cols_ap.rearrange("(k p) n -> p k n", p=128), [], [key])
            return W, key

        memset("pool", ident[:], 0.0, ["ident"])
        S.op("pool", lambda e: e.affine_select(out=ident[:], in_=ident[:], pattern=[[-1, 128]], compare_op=ALU.not_equal,
                                               fill=1.0, base=0, channel_multiplier=1), ["ident"], ["ident"])
        memset("pool", MASK4[:], 1.0, ["MASK4"])
        memset("pool", TRI[:], 1.0, ["TRI"])
        for dd in range(2):
            sg = 1 if dd == 0 else -1
            for sl in range(4):
                S.op("pool", lambda e, sl=sl, sg=sg, dd=dd: e.affine_select(
                    out=MASK4[:, dd, sl, :], in_=MASK4[:, dd, sl, :], pattern=[[sg, 128]], compare_op=ALU.is_ge,
                    fill=0.0, base=0, channel_multiplier=-sg), ["MASK4"], ["MASK4"])
        for sl in range(2):
            sg = 1 if sl == 0 else -1
            S.op("pool", lambda e, sl=sl, sg=sg: e.affine_select(
                out=TRI[:, sl, :], in_=TRI[:, sl, :], pattern=[[sg, 128]], compare_op=ALU.is_ge,
                fill=0.0, base=0, channel_multiplier=-sg), ["TRI"], ["TRI"])
        dma("sp", cvF[:], cvecF_d, [], ["cvF"])
        dma("sp", bmodF[:], bmodF_d, [], ["bmodF"])
        dma("sp", n1wF[:], n1wF_d, [], ["n1wF"])
        dma("sp", n2wF[:], n2wF_d, [], ["n2wF"])
        dma("sp", bcvF[:], bcvF_d, [], ["bcvF"])
        dma("sp", bmgF[:], bmgF_d, [], ["bmgF"])
        dma("sp", cwF[:], cwF_d, [], ["cwF"])
        dma("sp", BGB[:], bg_d.partition_broadcast(128), [], ["BGB"])
        dma("pool", WGS[:], wg_d.rearrange("(k p) n -> p k n", p=128), [], ["WGS"])
        tsmul("dve", bmgF[:], bmgF[:], 0.5, ["bmgF"], ["bmgF"])

        act(thF[:], cvF[:], AF.Tanh, ["cvF"], ["thF"], scale=0.5)
        memset("dve", S2[:], 0.0, ["S2"])
        stt("dve", S2[:, :, 0:3], thF[:], 1.0, cvF[:], ALU.add, ALU.mult, ["thF", "cvF", "S2"], ["S2"])

        for blk in (0, 1, 2, 3, 6, 7, 8, 9):
            W, wk = load_w(wmod_d[:, blk * 512:(blk + 1) * 512], 512)
            pi = nxt("pa", 2)
            pa = PA[pi]
            for jj in range(4):
                for k in range(8):
                    mm(pa[:, jj * 4:jj * 4 + 3], W[:, k, jj * 128:(jj + 1) * 128], S2[:, k, 0:3], k == 0, k == 7,
                       [wk, "S2"], [f"pa{pi}"])
            j0 = blk * 4
            stt("dve", modF[:, j0:j0 + 4, :], pa[:, 0:16].rearrange("p (a b) -> p a b", b=4)[:, :, 0:3], 0.5,
                bmodF[:, j0:j0 + 4].unsqueeze(2).to_broadcast([128, 4, 3]), ALU.mult, ALU.add,
                [f"pa{pi}", "bmodF"], ["modF"])
        stt("dve", A1[:], modF[:, 8:16, :], 1.0, n1wF[:].unsqueeze(2).to_broadcast([128, 8, 3]), ALU.add, ALU.mult,
            ["modF", "n1wF"], ["A1"])
        stt("dve", A2[:], modF[:, 32:40, :], 1.0, n2wF[:].unsqueeze(2).to_broadcast([128, 8, 3]), ALU.add, ALU.mult,
            ["modF", "n2wF"], ["A2"])

        SSB = sb("SSB", [128, 2, NTC], F32)

        def sumsq_tile(src, src_keys, i):
            if i % 2 == 0:
                act(XN[0][:], src, AF.Square, list(src_keys), ["XN0", f"SSB{i}"], accum_out=SSB[:, 0, i:i + 1])
            else:
                S.op("dve", lambda e: e.scalar_tensor_tensor(out=XN[1][:], in0=src, scalar=1.0, in1=src, op0=ALU.mult, op1=ALU.mult,
                                                             accum_out=SSB[:, 0, i:i + 1]), list(src_keys), ["XN1", f"SSB{i}"])

        def rstd_all(n):
            ks = [f"SSB{i}" for i in range(n)]
            ts2("dve", SSB[:, 1, 0:n], SSB[:, 0, 0:n], 1.0 / D, EPS, ALU.mult, ALU.add, ks, ["RSB"])
            act(SSB[:, 1, 0:n], SSB[:, 1, 0:n], AF.Ln, ["RSB"], ["RSB"])
            act(SSB[:, 1, 0:n], SSB[:, 1, 0:n], AF.Exp, ["RSB"], ["RSB"], scale=-0.5)

        def norm_apply_T(src, src_keys, i, dst, dst_key, tok0, Avec, SHvec, vec_keys):
            xn = XN[0]
            act(xn[:], src, AF.Copy, list(src_keys) + ["RSB"], ["XN0"], scale=SSB[:, 1, i:i + 1])
            pi = nxt("pt", 2)
            pt = PT[pi]
            for k in range(8):
                tr(pt[:, k, :], xn[:, k * 128:(k + 1) * 128], ["XN0"], [f"pt{pi}"])
            for k in range(8):
                if k % 4 == 3:
                    act(dst[:, k, tok0:tok0 + 128], pt[:, k, :], AF.Identity, [f"pt{pi}"] + list(vec_keys), [dst_key],
                        scale=Avec[:, k:k + 1], bias=SHvec[:, k:k + 1])
                else:
                    ts2("dve", dst[:, k, tok0:tok0 + 128], pt[:, k, :], Avec[:, k:k + 1], SHvec[:, k:k + 1], ALU.mult, ALU.add,
                        [f"pt{pi}"] + list(vec_keys), [dst_key])

        def stage(n):
            if n > kstop:
                raise _Stop()

        try:
          for b in range(NB):
              barrier()
              stage(1)
              def xsrc(i):
                  return x_d[b, i * 128:(i + 1) * 128, :] if i < NT else ctx_d[b, (i - NT) * 128:(i - NT + 1) * 128, :]
              for i in range(NTC):
                  dma("sp", XS[i], xsrc(i), [], [f"XS{i}"])
              for i in range(NTC):
                  sumsq_tile(XS[i], [f"XS{i}"], i)
              rstd_all(NTC)
              for i in range(NTC):
                  r = b if i < NT else 2
                  norm_apply_T(XS[i], [f"XS{i}"], i, hT, "hT", i * 128, A1[:, :, r], modF[:, 0:8, r], ["A1", "modF"])
              stage(2)
              for i in range(NTC):
                  pi = 0 if i < NT else 1
                  sl = i % NT
                  for k in range(8):
                      mm(PA[pi][:, sl * 32:(sl + 1) * 32], hT[:, k, i * 128:(i + 1) * 128], WGS[:, k, :], k == 0, k == 7,
                         ["hT", "WGS"], [f"pa{pi}"])
              tt("dve", G[:, 0:NT, :], PA[0][:].rearrange("p (c w) -> p c w", w=32), BGB[:].unsqueeze(1).to_broadcast([128, NT, 32]),
                 ALU.add, ["pa0", "BGB"], ["G"])
              tt("dve", G[:, NT:NTC, :], PA[1][:, 0:64].rearrange("p (c w) -> p c w", w=32),
                 BGB[:].unsqueeze(1).to_broadcast([128, 2, 32]), ALU.add, ["pa1", "BGB"], ["G"])
              rot["pa"] = 0
              act(NLF[:], G[:, :, 16:32], AF.Exp, ["G"], ["NLF"], scale=-1.0)
              tsadd("dve", NLF[:], NLF[:], 1.0, ["NLF"], ["NLF"])
              act(NLF[:], NLF[:], AF.Ln, ["NLF"], ["NLF"])
              for d in range(2):
                  def g3(nm):
                      return gt[(nm, d)][:].rearrange("p (c h) -> p c h", h=8)
                  cp("dve", g3("TMP"), NLF[:, :, d * 8:(d + 1) * 8], ["NLF"], [f"TMP{d}"])
                  mm(PA[0][:, 0:144], TRI[:, d, :], gt[("TMP", d)][:], True, True, [f"TMP{d}", "TRI"], ["pa0"])
                  mm(PA[1][:, 0:144], TRI[:, 2, :], gt[("TMP", d)][:], True, True, [f"TMP{d}", "TRI"], ["pa1"])
                  cp("act", gt[("NB", d)][:], PA[0][:, 0:144], ["pa0"], [f"NB{d}"])
                  cp("act", gt[("NBL", d)][:], PA[1][:, 0:144], ["pa1"], [f"NBL{d}"])
                  act(gt[("EQ", d)][:], gt[("NB", d)][:], AF.Exp, [f"NB{d}"], [f"EQ{d}"], scale=-1.0)
                  tt("dve", g3("TMP"), G[:, :, d * 8:(d + 1) * 8], g3("NB"), ALU.add, ["G", f"NB{d}"], [f"TMP{d}"])
                  act(gt[("EK", d)][:], gt[("TMP", d)][:], AF.Exp, [f"TMP{d}"], [f"EK{d}"])
                  tsmul("dve", gt[("EK", d)][:], gt[("EK", d)][:], 0.125, [f"EK{d}"], [f"EK{d}"])
                  tt("dve", gt[("TMP", d)][:], gt[("TMP", d)][:], gt[("NBL", d)][:], ALU.subtract, [f"TMP{d}", f"NBL{d}"], [f"TMP{d}"])
                  act(gt[("EKH", d)][:], gt[("TMP", d)][:], AF.Exp, [f"TMP{d}"], [f"EKH{d}"])
                  tsmul("dve", gt[("EKH", d)][:], gt[("EKH", d)][:], 0.125, [f"EKH{d}"], [f"EKH{d}"])
                  act(gt[("DEC", d)][:], gt[("NBL", d)][:], AF.Exp, [f"NBL{d}"], [f"DEC{d}"], scale=-1.0)

              cp("dve", S2rep, S2[:, :, b:b + 1].to_broadcast([128, 8, 128]), ["S2"], ["S2rep"])
              for (blk0, dst, dkey, c_ps, c_b) in ((4, G1H, "G1H", 0.25, 0.5), (10, G2, "G2", 0.5, 1.0)):
                  fi = nxt("f4k", 2)
                  bb = F4K[fi]
                  dma("sp", bb[:], bmod_d[blk0 * 512:blk0 * 512 + 1024].partition_broadcast(128), [], [f"F4K{fi}"])
                  for hb in range(2):
                      W, wk = load_w(wmod_d[:, (blk0 + hb) * 512:(blk0 + hb + 1) * 512], 512)
                      pi = nxt("pa", 2)
                      pa = PA[pi]
                      for k in range(8):
                          mm(pa[:], S2rep[:, k, :], W[:, k, :], k == 0, k == 7, [wk, "S2rep"], [f"pa{pi}"])
                      tsmul("dve", dst[:, hb * 512:(hb + 1) * 512], pa[:], c_ps, [f"pa{pi}"], [dkey])
                      stt("dve", dst[:, hb * 512:(hb + 1) * 512], bb[:, hb * 512:(hb + 1) * 512], c_b,
                          dst[:, hb * 512:(hb + 1) * 512], ALU.mult, ALU.add, [f"F4K{fi}", dkey], [dkey])

              stage(3)
              orders = ([16, 17] + list(range(16)), [17, 16] + list(range(15, -1, -1)))
              tmk = [f"TM{i}" for i in range(NTC)]
              S.op("pool", lambda e: e.memset(TM[:, :, 384:385], 1.0), ["hT"], ["TM1"])
              pending = []

              def build_post(h, mi):
                  P = []
                  P.append(lambda: stt("dve", AD[:], HN[:, :, :, 128], -1.0, HN[:, :, :, 128], ALU.mult, ALU.max, ["HN"], ["AD"]))
                  P.append(lambda: tsmax("dve", AD[:], AD[:], 1.0, ["AD"], ["AD"]))
                  P.append(lambda: S.op("dve", lambda e: e.reciprocal(out=RD[:], in_=AD[:]), ["AD"], ["RD"]))
                  P.append(lambda: tt("dve", HS[:], HN[:, :, 0, 0:128], RD[:, :, 0:1].to_broadcast([128, NT, 128]), ALU.mult,
                                      ["HN", "RD"], ["HS", "KQT"]))
                  P.append(lambda: tt("dve", SQ[:], HN[:, :, 1, 0:128], RD[:, :, 1:2].to_broadcast([128, NT, 128]), ALU.mult,
                                      ["HN", "RD"], ["SQ", "CSf0", "CSf1"]))
                  P.append(lambda: tt("dve", HS[:], HS[:], SQ[:], ALU.add, ["HS", "SQ"], ["HS", "KQT"]))
                  P.append(lambda: tt("dve", SQ[:], HS[:], HS[:], ALU.mult, ["HS"], ["SQ", "CSf0", "CSf1"]))
                  P.append(lambda: S.op("dve", lambda e: e.tensor_reduce(out=SSQ[:], in_=SQ[:], axis=AX.X, op=ALU.add), ["SQ"], ["SSQ"]))
                  P.append(lambda: ts2("dve", SSQ[:], SSQ[:], 1.0 / 128, EPS, ALU.mult, ALU.add, ["SSQ"], ["SSQ"]))
                  P.append(lambda: act(SSQ[:], SSQ[:], AF.Ln, ["SSQ"], ["SSQ"]))
                  P.append(lambda: act(SSQ[:], SSQ[:], AF.Exp, ["SSQ"], ["SSQ"], scale=-0.5))
                  P.append(lambda: tt("dve", HS[:], HS[:], SSQ[:].unsqueeze(2).to_broadcast([128, NT, 128]), ALU.mult,
                                      ["HS", "SSQ"], ["HS", "KQT"]))
                  P.append(lambda: tt("dve", HS[:], HS[:], MW[mi][:].unsqueeze(1).to_broadcast([128, NT, 128]), ALU.mult,
                                      ["HS", f"MW{mi}"], ["HS", "KQT"]))
                  P.append(lambda: stt("dve", GT[:], TO[:], 1.0, HS[:], ALU.add, ALU.mult, ["TO", "HS"], ["GT", "CSf0", "CSf1", "CS"]))

                  def trs(i0):
                      pi = nxt("pt", 2)
                      for ii in range(8):
                          tr(PT[pi][:, ii, :], GT[:, i0 + ii, :], ["GT"], [f"pt{pi}"])
                      cp("act", hsT[:, h, i0 * 128:(i0 + 8) * 128].rearrange("p (a t) -> p a t", t=128), PT[pi][:],
                         [f"pt{pi}"], ["hsT"])
                  P.append(lambda: trs(0))
                  P.append(lambda: trs(8))
                  return P

              for h, (W, wk) in wloop(H, lambda hh: wtm_d[:, hh * 384:(hh + 1) * 384], 384):
                  if h >= 1:
                      stage(4)
                  bi = nxt("bt", 1)
                  dma("sp", BT[bi][:], btm_d[h * 384:(h + 1) * 384].partition_broadcast(128), [], [f"BT{bi}"])
                  mi = nxt("mw", 2)
                  dma("sp", MW[mi][:], mnw_d[h * 128:(h + 1) * 128].partition_broadcast(128), [], [f"MW{mi}"])
                  tsmul("dve", MW[mi][:], MW[mi][:], 0.5, [f"MW{mi}"], [f"MW{mi}"])
                  for i in range(NTC):
                      pa_ap, pa_k = PAX[nxt("pax4", 4)]
                      for k in range(8):
                          mm(pa_ap[:, 0:384], hT[:, k, i * 128:(i + 1) * 128], W[:, k, 0:384], k == 0, k == 7,
                             ["hT", wk], [pa_k])
                      tt("dve", TM[:, i, 0:384], pa_ap[:, 0:384], BT[bi][:], ALU.add, [pa_k, f"BT{bi}"], [f"TM{i}"])
                      if pending:
                          pending.pop(0)()
                  while pending:
                      pending.pop(0)()
                  stage(3.1)
                  def g3d(nm, d):
                      return gt[(nm, d)][:].rearrange("p (c h) -> p c h", h=8)
                  for d in range(2):
                      se = "pool" if d == 0 else "dve"
                      tt(se, KH[:, :, d, :], TM[:, :, 0:64], g3d("EKH", d)[:, :, h:h + 1].to_broadcast([128, NTC, 64]), ALU.mult,
                         tmk + [f"EKH{d}"], [f"KH{d}", "HN"])
                  for d in range(2):
                      se = "pool" if d == 0 else "dve"
                      tt(se, SCL[:, :, d, :], TM[:, :, 0:64], g3d("EK", d)[:, :, h:h + 1].to_broadcast([128, NTC, 64]), ALU.mult,
                         tmk + [f"EK{d}"], [f"SCL{d}", "HN"])
                  for d in range(2):
                      tt("dve", SCL[:, 0:NT, 2 + d, :], TM[:, 0:NT, 64:128], g3d("EQ", d)[:, 0:NT, h:h + 1].to_broadcast([128, NT, 64]),
                         ALU.mult, tmk + [f"EQ{d}"], [f"SCLQ{d}", "HN"])
                  stage(3.3)
                  for i0 in range(0, NTC, 2):
                      pi = nxt("pn", 2)
                      for ii in range(2):
                          i = i0 + ii
                          mm(PN[pi][:, ii, 0:129], KH[:, i].rearrange("p a b -> p (a b)"), TM[:, i, 256:385], True, True,
                             ["KH0", "KH1", f"TM{i}", "TM1"], [f"pn{pi}"])
                      cp("act", DC[:, i0:i0 + 2, :], PN[pi][:, :, 0:129], [f"pn{pi}"], ["DC", "HN", "TO"])
                  stage(3.4)
                  decs = [gt[("DEC", d)][:].rearrange("p (c h) -> p c h", h=8) for d in range(2)]
                  for d in range(2):
                      rows = slice(d * 64, (d + 1) * 64)
                      memset("dve", CSf[rows, orders[d][0], :], 0.0, [f"CSf{d}", "SQ", "GT"])
                  for kk in range(1, NTC):
                      for d in range(2):
                          rows = slice(d * 64, (d + 1) * 64)
                          c_prev, c = orders[d][kk - 1], orders[d][kk]
                          stt("dve", CSf[rows, c, :], CSf[rows, c_prev, :], decs[d][rows, c_prev, h:h + 1], DC[rows, c_prev, :],
                              ALU.mult, ALU.add, [f"CSf{d}", f"DEC{d}", "DC"], [f"CSf{d}"])
                  stage(3.2)
                  for i0 in range(0, NT, 4):
                      pi = nxt("pt", 2)
                      for ii in range(4):
                          i = i0 + ii
                          tr(PT[pi][:, 2 * ii, :], SCL[:, i, 0:2, :].rearrange("p a b -> p (a b)"), ["SCL0", "SCL1"], [f"pt{pi}"])
                          tr(PT[pi][:, 2 * ii + 1, :], SCL[:, i, 2:4, :].rearrange("p a b -> p (a b)"), ["SCLQ0", "SCLQ1"], [f"pt{pi}"])
                      cp("act", KQT[:, i0:i0 + 4].rearrange("p c a t -> p (c a) t"), PT[pi][:], [f"pt{pi}"], ["KQT", "HS"])
                  stage(3.45)
                  cp("act", CS[:, :, 0:129], CSf[:, 0:NT, :], ["CSf0", "CSf1"], ["CS", "GT"])
                  act(TO[:], TM[:, 0:NT, 128:256], AF.Tanh, tmk, ["TO", "DC"], scale=0.5)
                  stage(3.5)
                  def scores(i0):
                      for ii in range(4):
                          i = i0 + ii
                          for d in range(2):
                              rows = slice(d * 64, (d + 1) * 64)
                              mm(PS[d][:, ii, :], KQT[rows, i, 0, :], KQT[rows, i, 1, :], True, True, ["KQT"], [f"ps{d}"])
                      pq = nxt("pp", 2)
                      for d in range(2):
                          tt("dve", PP[pq * 2 + d][:], PS[d][:], MASK4[:, d], ALU.mult, [f"ps{d}", "MASK4"], [f"PP{pq * 2 + d}"])
                      return pq

                  def outputs(i0, pq):
                      for ii in range(4):
                          i = i0 + ii
                          pi = nxt("pn", 2)
                          for d in range(2):
                              rows = slice(d * 64, (d + 1) * 64)
                              mm(PN[pi][:, d, 0:129], PP[pq * 2 + d][:, ii, :], TM[:, i, 256:385], True, False,
                                 [f"PP{pq * 2 + d}", f"TM{i}", "TM1"], [f"pn{pi}"])
                              mm(PN[pi][:, d, 0:129], KQT[rows, i, 1, :], CS[rows, i, 0:129], False, True,
                                 ["KQT", "CS"], [f"pn{pi}"])
                          cp("act", HN[:, i, :, :], PN[pi][:, :, 0:129], [f"pn{pi}"],
                             ["HN", "SCL0", "SCL1", "SCLQ0", "SCLQ1", "KH0", "KH1", "DC"])

                  pq_prev = scores(0)
                  for i0 in range(0, NT, 4):
                      pq_next = scores(i0 + 4) if i0 + 4 < NT else None
                      outputs(i0, pq_prev)
                      pq_prev = pq_next
                  stage(3.6)
                  pending = build_post(h, mi)
              while pending:
                  pending.pop(0)()

              stage(5)
              barrier()
              for c, (W, wk) in wloop(8, lambda cc: wcv_d[:, cc * 384:(cc + 1) * 384], 384):
                  for g in range(4):
                      tsl = slice(g * 512, (g + 1) * 512)
                      for part in range(3):
                          pa_ap, pa_k = pax()
                          for k in range(8):
                              mm(pa_ap, W[:, k, part * 128:(part + 1) * 128], hT[:, k, tsl], k == 0, k == 7, ["hT", wk], [pa_k])
                          act(X3[part], pa_ap, AF.Identity, [pa_k, "bcvF"], [f"X3{part}"],
                              bias=bcvF[:, c * 3 + part:c * 3 + part + 1])
                      tt("pool", CU, X3[1], X3[0], ALU.mult, ["X31", "X30"], ["CU"])
                      tsmul("dve", CA, CU, cwF[:, c, 1:2], ["CU", "cwF"], ["CA"])
                      U3 = CU.rearrange("p (r w) -> p r w", w=64)
                      A3 = CA.rearrange("p (r w) -> p r w", w=64)
                      stt("dve", A3[:, :, 1:64], U3[:, :, 0:63], cwF[:, c, 0:1], A3[:, :, 1:64], ALU.mult, ALU.add, ["CU", "CA", "cwF"], ["CA"])
                      stt("dve", A3[:, :, 0:63], U3[:, :, 1:64], cwF[:, c, 2:3], A3[:, :, 0:63], ALU.mult, ALU.add, ["CU", "CA", "cwF"], ["CA"])
                      tt("pool", aT[:, c, tsl], CA, X3[2], ALU.mult, ["CA", "X32"], ["aT"])

              stage(6)
              barrier()
              for j, (W, wk) in wloop(8, lambda jj: wpost_d[:, jj * 512:(jj + 1) * 512], 512):
                  for g in range(4):
                      tsl = slice(g * 512, (g + 1) * 512)
                      for (which, src, skey, Tt, Mm) in ((0, aT, "aT", T1, M1), (1, hsT, "hsT", T2, M2)):
                          pa_ap, pa_k = pax()
                          for k in range(8):
                              mm(pa_ap, W[:, k, 256 + which * 128:384 + which * 128], hT[:, k, tsl], k == 0, k == 7, ["hT", wk], [pa_k])
                          act(Tt, pa_ap, AF.Tanh, [pa_k, "bmgF"], [f"T{which}"], scale=0.5,
                              bias=bmgF[:, j * 2 + which:j * 2 + which + 1])
                          pa_ap, pa_k = pax()
                          for k in range(8):
                              mm(pa_ap, W[:, k, which * 128:(which + 1) * 128], src[:, k, tsl], k == 0, k == 7, [skey, wk], [pa_k])
                          stt("dve", Mm, Tt, 1.0, pa_ap, ALU.add, ALU.mult, [f"T{which}", pa_k], [f"M{which}"])
                      tt("pool", mT[:, j, tsl], M1, M2, ALU.add, ["M0", "M1"], ["mT"])

              stage(7)
              barrier()
              for k in range(8):
                  fi = nxt("f4k", 2)
                  dma("sp", F4K[fi][:], wout_d[k * 128:(k + 1) * 128, :], [], [f"F4K{fi}"])
                  tt("dve", WO[:, k, :], F4K[fi][:], G1H[:], ALU.mult, [f"F4K{fi}", "G1H"], ["WO"])
              for i in range(NT):
                  dma("sp", X1[:, i, :], x_d[b, i * 128:(i + 1) * 128, :], [], [f"X1_{i}"])
                  for nb in range(2):
                      pa_ap, pa_k = pax()
                      for k in range(8):
                          mm(pa_ap, mT[:, k, i * 128:(i + 1) * 128], WO[:, k, nb * 512:(nb + 1) * 512], k == 0, k == 7, ["mT", "WO"], [pa_k])
                      tt("dve", X1[:, i, nb * 512:(nb + 1) * 512], pa_ap, X1[:, i, nb * 512:(nb + 1) * 512], ALU.add,
                         [pa_k, f"X1_{i}"], [f"X1_{i}"])
                  sumsq_tile(X1[:, i, :], [f"X1_{i}"], i)
              rstd_all(NT)
              barrier()
              for i in range(NT):
                  norm_apply_T(X1[:, i, :], [f"X1_{i}"], i, h2T, "h2T", i * 128, A2[:, :, b], modF[:, 24:32, b], ["A2", "modF"])

              stage(8)
              for qd, (W1, w1k) in wloop(NQ, lambda qq: wff1_d[:, qq * 512:(qq + 1) * 512], 512):
                  par = qd % 2
                  for fl in range(FQ):
                      fi = nxt("f4k", 2)
                      f = qd * FQ + fl
                      dma("sp", F4K[fi][:], wff2_d[f * 128:(f + 1) * 128, :], [], [f"F4K{fi}"])
                      tt("pool", W2S[par][:, fl, :], F4K[fi][:], G2[:], ALU.mult, [f"F4K{fi}", "G2"], [f"W2S{par}"])
                  for fl in range(FQ):
                      for g in range(4):
                          tsl = slice(g * 512, (g + 1) * 512)
                          pa_ap, pa_k = pax()
                          for k in range(8):
                              mm(pa_ap, W1[:, k, fl * 128:(fl + 1) * 128], h2T[:, k, tsl], k == 0, k == 7, ["h2T", w1k], [pa_k])
                          ri = nxt("rr", 2)
                          act(RR[ri], pa_ap, AF.Relu, [pa_k], [f"RR{ri}"])
                          tt("pool", AQ[:, fl, tsl], RR[ri], RR[ri], ALU.mult, [f"RR{ri}"], ["AQ"])
                  for i in range(NT):
                      for nb in range(2):
                          pa_ap, pa_k = pax()
                          for fl in range(FQ):
                              mm(pa_ap, AQ[:, fl, i * 128:(i + 1) * 128], W2S[par][:, fl, nb * 512:(nb + 1) * 512], fl == 0, fl == FQ - 1,
                                 ["AQ", f"W2S{par}"], [pa_k])
                          tt("dve", X1[:, i, nb * 512:(nb + 1) * 512], pa_ap, X1[:, i, nb * 512:(nb + 1) * 512], ALU.add,
                             [pa_k, f"X1_{i}"], [f"X1_{i}"])
                      if qd == NQ - 1:
                          sumsq_tile(X1[:, i, :], [f"X1_{i}"], i)

              stage(9)
              barrier()
              dma("sp", FNWB, fnw_d.partition_broadcast(128), [], ["FNWB"])
              rstd_all(NT)
              for i in range(NT):
                  fi = nxt("f4k", 2)
                  stt("dve", F4K[fi][:], X1[:, i, :], SSB[:, 1, i:i + 1], FNWB, ALU.mult, ALU.mult, [f"X1_{i}", "RSB", "FNWB"], [f"F4K{fi}"])
                  dma("sp", out_d[b, i * 128:(i + 1) * 128, :], F4K[fi][:], [f"F4K{fi}"], [])
        except _Stop:
            pass

        S.emit({"pe": block.tensor, "act": block.scalar, "dve": block.vector, "pool": block.gpsimd, "sp": block.sync}, sems, dsems)
    return nc


def _fm(v, nchunk):
    return np.ascontiguousarray(np.asarray(v, np.float32).reshape(nchunk, 128).T)


def kernel(x, c, ctx, c_ctx, w_mod, b_mod, norm1_w, w_in, b_in, conv_w, mlstm_norm_w,
           w_conv_out, w_mlstm_out, w_out, norm2_w, w_ff1, w_ff2, final_norm_w):
    f32 = np.float32
    x = np.asarray(x, f32)
    c = np.asarray(c, f32)
    ctx = np.asarray(ctx, f32)
    c_ctx = np.asarray(c_ctx, f32)
    w_in0 = np.asarray(w_in, f32)[0]
    b_in0 = np.asarray(b_in, f32)[0]
    B = x.shape[0]
    NB = B // N_CORES

    o_k, o_v, o_ig, o_fg, o_q, o_o = 0, 512, 1536, 1552, 1568, 2080
    o_xin, o_gc, o_gb, o_mg = 3104, 4128, 5152, 6176
    cols_tm, cols_cv, cols_mg = [], [], []
    for h in range(H):
        cols_tm += list(range(o_k + h * 64, o_k + (h + 1) * 64))
        cols_tm += list(range(o_q + h * 64, o_q + (h + 1) * 64))
        cols_tm += list(range(o_o + h * 128, o_o + (h + 1) * 128))
        cols_tm += list(range(o_v + h * 128, o_v + (h + 1) * 128))
    for ch in range(8):
        for base in (o_xin, o_gc, o_gb):
            cols_cv += list(range(base + ch * 128, base + (ch + 1) * 128))
    cols_g = list(range(o_ig, o_ig + 16)) + list(range(o_fg, o_fg + 16))
    wco = np.asarray(w_conv_out, f32)[0]
    wmo = np.asarray(w_mlstm_out, f32)[0]
    wpost = np.concatenate(
        [np.concatenate([wco[:, j * 128:(j + 1) * 128], wmo[:, j * 128:(j + 1) * 128],
                         w_in0[:, o_mg + j * 128:o_mg + (j + 1) * 128],
                         w_in0[:, o_mg + 1024 + j * 128:o_mg + 1024 + (j + 1) * 128]], axis=1) for j in range(8)], axis=1)
    bcvF = np.stack([b_in0[base + ch * 128:base + (ch + 1) * 128] for ch in range(8) for base in (o_xin, o_gc, o_gb)], axis=1)
    bmgF = np.stack([b_in0[o_mg + w * 1024 + j * 128:o_mg + w * 1024 + (j + 1) * 128] for j in range(8) for w in range(2)], axis=1)
    shared = {
        "wmod": np.ascontiguousarray(np.asarray(w_mod, f32)[0]),
        "bmodF": _fm(np.asarray(b_mod, f32)[0], 48),
        "bmod": np.ascontiguousarray(np.asarray(b_mod, f32)[0]),
        "n1wF": _fm(np.asarray(norm1_w, f32)[0], 8),
        "n2wF": _fm(np.asarray(norm2_w, f32)[0], 8),
        "wg": np.ascontiguousarray(w_in0[:, cols_g]),
        "wtm": np.ascontiguousarray(w_in0[:, cols_tm]),
        "wcv": np.ascontiguousarray(w_in0[:, cols_cv]),
        "wpost": np.ascontiguousarray(wpost),
        "bg": np.ascontiguousarray(b_in0[cols_g]),
        "btm": np.ascontiguousarray(b_in0[cols_tm]),
        "bcvF": np.ascontiguousarray(bcvF),
        "bmgF": np.ascontiguousarray(bmgF),
        "cwF": np.ascontiguousarray(np.asarray(conv_w, f32)[0].reshape(3, 8, 128).transpose(2, 1, 0)),
        "mnw": np.ascontiguousarray(np.asarray(mlstm_norm_w, f32)[0]),
        "wout": np.ascontiguousarray(np.asarray(w_out, f32)[0]),
        "wff1": np.ascontiguousarray(np.asarray(w_ff1, f32)[0]),
        "wff2": np.ascontiguousarray(np.asarray(w_ff2, f32)[0]),
        "fnw": np.ascontiguousarray(np.asarray(final_norm_w, f32)),
    }
    in_maps = []
    for core in range(N_CORES):
        bs = slice(core * NB, (core + 1) * NB)
        cv = np.concatenate([c[bs], c_ctx[None, :]], axis=0)
        assert NB == 2
        m = dict(shared)
        m["x"] = np.ascontiguousarray(x[bs])
        m["ctx"] = np.ascontiguousarray(ctx[bs])
        m["cvecF"] = np.ascontiguousarray(cv.reshape(3, 8, 128).transpose(2, 1, 0))
        in_maps.append(m)
    nc = build_program(NB)
    res = run_bass_kernel_spmd(nc, in_maps, core_ids=list(range(N_CORES)))
    out = np.concatenate([np.asarray(r["out"], f32) for r in res.results], axis=0)
    return out
```

```python
import math
from contextlib import ExitStack

import numpy as np
import concourse.bass as bass
import concourse.mybir as mybir
from concourse.bass_utils import run_bass_kernel_spmd

F32 = mybir.dt.float32
BF16 = mybir.dt.bfloat16
AF = mybir.ActivationFunctionType
ALU = mybir.AluOpType
AX = mybir.AxisListType

D = 1024
T = 2048
CT = 256
NT = 16
NTC = 18
H = 8
EPS = 1e-6
N_CORES = 8
NQ = 8
FQ = 32 // NQ

ENGS = ("pe", "act", "dve", "pool", "sp")
N_DMA_SEMS = 8
import os
KVAR = int(os.environ.get('KVAR', '0'))


class Op:
    __slots__ = ("eng", "fn", "dma", "deps", "signal", "sig_sem", "sig_val", "idx")

    def __init__(self, eng, fn, dma):
        self.eng = eng
        self.fn = fn
        self.dma = dma
        self.deps = set()
        self.signal = False
        self.sig_sem = None
        self.sig_val = 0


class Sched:
    def __init__(self):
        self.ops = []
        self.eops = {e: [] for e in ENGS}
        self.last_w = {}
        self.readers = {}
        self.epoch_op = None
        self.dma_since = []

    def barrier(self, eng, fn):
        o = Op(eng, fn, False)
        o.idx = len(self.ops)
        for e in ENGS:
            for p in reversed(self.eops[e]):
                if not p.dma:
                    o.deps.add(p)
                    break
        for p in self.dma_since:
            o.deps.add(p)
        if self.epoch_op is not None:
            o.deps.add(self.epoch_op)
        self.dma_since = []
        self.epoch_op = o
        self.ops.append(o)
        self.eops[eng].append(o)
        return o

    def op(self, eng, fn, reads=(), writes=(), dma=False, epoch=True):
        o = Op(eng, fn, dma)
        o.idx = len(self.ops)
        reads = list(reads)
        if epoch and self.epoch_op is not None:
            o.deps.add(self.epoch_op)
        if dma:
            self.dma_since.append(o)
        for k in reads:
            w = self.last_w.get(k)
            if w is not None:
                o.deps.add(w)
        for k in writes:
            w = self.last_w.get(k)
            if w is not None:
                o.deps.add(w)
            for r in self.readers.get(k, {}).values():
                o.deps.add(r)
        for k in reads:
            self.readers.setdefault(k, {})[("dma", o.idx) if dma else eng] = o
        for k in writes:
            self.last_w[k] = o
            self.readers[k] = {}
        o.deps.discard(o)
        self.ops.append(o)
        self.eops[eng].append(o)
        return o

    def emit(self, block_engines, sems, dma_sems):
        for o in self.ops:
            nd = set()
            for d in o.deps:
                if (not d.dma) and (not o.dma) and d.eng == "pe" and o.eng == "pe":
                    continue
                nd.add(d)
                d.signal = True
            o.deps = nd
        cnt = {e: 0 for e in ENGS}
        dcnt = {e: 0 for e in ENGS}
        dsem_val = {}
        prev_on_sem = {}
        for e in ENGS:
            for o in self.eops[e]:
                if o.dma:
                    i = dcnt[e] % N_DMA_SEMS
                    dcnt[e] += 1
                    dsem_val[(e, i)] = dsem_val.get((e, i), 0) + 16
                    o.sig_sem = dma_sems[e][i]
                    o.sig_val = dsem_val[(e, i)]
                    p = prev_on_sem.get((e, i))
                    if p is not None:
                        o.deps.add(p)
                    prev_on_sem[(e, i)] = o
                    o.signal = True
                elif o.signal:
                    cnt[e] += 1
                    o.sig_sem = sems[e]
                    o.sig_val = cnt[e]
        finals = list(prev_on_sem.values())

        def run(eng_name):
            def body(eng):
                waited = {}
                for o in self.eops[eng_name]:
                    need = {}
                    for d in o.deps:
                        k = id(d.sig_sem)
                        if k not in need or need[k][1] < d.sig_val:
                            need[k] = (d.sig_sem, d.sig_val)
                    for k, (s, v) in need.items():
                        if waited.get(k, 0) >= v:
                            continue
                        eng.wait_ge(s, v)
                        waited[k] = v
                    ins = o.fn(eng)
                    if o.signal:
                        ins.then_inc(o.sig_sem, 16 if o.dma else 1)
                if eng_name == "sp":
                    for o in finals:
                        k = id(o.sig_sem)
                        if waited.get(k, 0) >= o.sig_val:
                            continue
                        eng.wait_ge(o.sig_sem, o.sig_val)
                        waited[k] = o.sig_val
            return body

        for e in ENGS:
            block_engines[e](run(e))


class _Stop(Exception):
    pass


def build_program(NB, kstop=99):
    nc = bass.Bass("TRN2", target_bir_lowering=False)

    def dram(name, shape, kind="ExternalInput"):
        return nc.dram_tensor(name, list(shape), F32, kind=kind).ap()

    x_d = dram("x", [NB, T, D])
    ctx_d = dram("ctx", [NB, CT, D])
    cvecF_d = dram("cvecF", [128, 8, 3])
    wmod_d = dram("wmod", [D, 6 * D])
    bmodF_d = dram("bmodF", [128, 48])
    bmod_d = dram("bmod", [6 * D])
    n1wF_d = dram("n1wF", [128, 8])
    n2wF_d = dram("n2wF", [128, 8])
    wg_d = dram("wg", [D, 32])
    wtm_d = dram("wtm", [D, 8 * 384])
    wcv_d = dram("wcv", [D, 8 * 384])
    wpost_d = dram("wpost", [D, 8 * 512])
    bg_d = dram("bg", [32])
    btm_d = dram("btm", [8 * 384])
    bcvF_d = dram("bcvF", [128, 24])
    bmgF_d = dram("bmgF", [128, 16])
    cwF_d = dram("cwF", [128, 8, 3])
    mnw_d = dram("mnw", [D])
    wout_d = dram("wout", [D, D])
    wff1_d = dram("wff1", [D, 4 * D])
    wff2_d = dram("wff2", [4 * D, D])
    fnw_d = dram("fnw", [D])
    out_d = dram("out", [NB, T, D], kind="ExternalOutput")

    S = Sched()
    with ExitStack() as es:
        def sb(name, shape, dt):
            return es.enter_context(nc.sbuf_tensor(name, list(shape), dt))

        def pst(name, shape, dt):
            return es.enter_context(nc.psum_tensor(name, list(shape), dt))

        BIGB = 143360
        BIG = sb("BIG", [128, BIGB // 2], BF16)

        def bview(off_bytes, nbytes, dt=BF16):
            v = BIG[:, off_bytes // 2:(off_bytes + nbytes) // 2]
            if dt == F32:
                v = v.bitcast(F32)
            return v

        hT = bview(0, 36864).rearrange("p (k t) -> p k t", k=8)
        hsT = bview(36864, 32768).rearrange("p (k t) -> p k t", k=8)
        aT = bview(69632, 32768).rearrange("p (k t) -> p k t", k=8)
        mT = bview(102400, 32768).rearrange("p (k t) -> p k t", k=8)
        SCR = 135168
        A0 = 69632
        TM = bview(A0, 18 * 386 * 2).rearrange("p (c w) -> p c w", w=386)
        o1 = A0 + 18 * 386 * 2
        SCL = bview(o1, 9216).rearrange("p (c a b) -> p c a b", a=4, b=64)
        KH = bview(o1 + 9216, 4608).rearrange("p (c a b) -> p c a b", a=2, b=64)
        DC = bview(o1 + 13824, 9288, F32).rearrange("p (c w) -> p c w", w=129)
        HN = bview(o1, 16512, F32).rearrange("p (c a w) -> p c a w", a=2, w=129)
        TO = bview(o1 + 16512, 4096).rearrange("p (c w) -> p c w", w=128)
        o2 = o1 + 23112
        KQT = bview(o2, 8192).rearrange("p (c a t) -> p c a t", a=2, t=128)
        CSf = bview(o2 + 8192, 9288, F32).rearrange("p (c w) -> p c w", w=129)
        CS = bview(o2 + 17480, 4160).rearrange("p (c w) -> p c w", w=130)
        HS = bview(o2, 8192, F32).rearrange("p (c w) -> p c w", w=128)
        SQ = bview(o2 + 8192, 8192, F32).rearrange("p (c w) -> p c w", w=128)
        GT = bview(o2 + 16384, 4096).rearrange("p (c w) -> p c w", w=128)
        o3 = o2 + 21640
        PP = [bview(o3 + i * 1024, 1024).rearrange("p (a t) -> p a t", a=4) for i in range(4)]
        assert o3 + 4096 <= 135168
        X3 = [bview(102400 + i * 2048, 2048, F32) for i in range(3)]
        CU = bview(102400 + 6144, 2048, F32)
        CA = bview(102400 + 8192, 2048, F32)
        T1 = bview(SCR, 2048, F32)
        T2 = bview(SCR + 2048, 2048, F32)
        M1 = bview(SCR + 4096, 2048, F32)
        M2 = bview(SCR + 6144, 2048, F32)
        X1 = bview(0, 65536, F32).rearrange("p (c w) -> p c w", w=1024)
        WO = bview(65536, 16384).rearrange("p (k n) -> p k n", k=8)
        h2T = bview(65536, 32768).rearrange("p (k t) -> p k t", k=8)
        AQ = bview(102400, 16384).rearrange("p (f t) -> p f t", f=FQ)
        W2S = [bview(118784 + i * 8192, 8192).rearrange("p (f n) -> p f n", f=FQ) for i in range(2)]
        RR = [bview(SCR + i * 2048, 2048, F32) for i in range(2)]

        XS = [bview(69632 + i * 4096, 4096, F32) for i in range(NTC)]
        WB = [sb(f"WB{i}", [128, 8, 512], BF16) for i in range(3)]
        F4K = [sb(f"F4K{i}", [128, 1024], F32) for i in range(2)]
        XN = [sb(f"XN{i}", [128, 1024], BF16) for i in range(2)]
        G1H = sb("G1H", [128, 1024], F32)
        G2 = sb("G2", [128, 1024], F32)
        FNWB = bview(65536, 4096, F32)
        ident = sb("ident", [128, 128], BF16)
        MASK4 = sb("MASK4", [128, 2, 4, 128], BF16)
        TRI = sb("TRI", [128, 3, 128], F32)
        cvF = sb("cvF", [128, 8, 3], F32)
        thF = sb("thF", [128, 8, 3], F32)
        S2 = sb("S2", [128, 8, 4], BF16)
        S2rep = bview(36864, 2048).rearrange("p (k m) -> p k m", k=8)
        bmodF = sb("bmodFs", [128, 48], F32)
        n1wF = sb("n1wFs", [128, 8], F32)
        n2wF = sb("n2wFs", [128, 8], F32)
        modF = sb("modF", [128, 48, 3], F32)
        A1 = sb("A1", [128, 8, 3], F32)
        A2 = sb("A2", [128, 8, 3], F32)
        bcvF = sb("bcvFs", [128, 24], F32)
        bmgF = sb("bmgFs", [128, 16], F32)
        cwF = sb("cwFs", [128, 8, 3], F32)
        WGS = sb("WGS", [128, 8, 32], BF16)
        BGB = sb("BGB", [128, 32], F32)
        BT = [sb(f"BT{i}", [128, 384], F32) for i in range(1)]
        MW = [sb(f"MW{i}", [128, 128], F32) for i in range(2)]
        G = sb("G", [128, NTC, 32], F32)
        NLF = sb("NLF", [128, NTC, 16], F32)
        gt = {}
        for d in range(2):
            for nm in ("NB", "NBL", "EQ", "EK", "EKH", "DEC", "TMP"):
                gt[(nm, d)] = sb(f"{nm}{d}", [128, NTC * 8], F32)
        AD = sb("AD", [128, 16, 2], F32)
        RD = sb("RD", [128, 16, 2], F32)
        SSQ = sb("SSQ", [128, 16], F32)
        DUM = sb("DUM", [128, 2], F32)

        PA = [pst(f"pa{i}", [128, 512], F32) for i in range(2)]
        PT = [pst(f"pt{i}", [128, 8, 128], BF16) for i in range(2)]
        PS = [pst(f"ps{i}", [128, 4, 128], F32) for i in range(2)]
        PN = [pst(f"pn{i}", [128, 2, 256], F32) for i in range(2)]

        sems = {e: es.enter_context(nc.semaphore("s_" + e)) for e in ENGS}
        dsems = {e: [es.enter_context(nc.semaphore(f"d_{e}{i}")) for i in range(N_DMA_SEMS)] for e in ("sp", "pool", "act")}
        block = es.enter_context(nc.Block())

        rot = {}

        def nxt(name, n):
            v = rot.get(name, 0)
            rot[name] = v + 1
            return v % n

        def mm(out, lhsT, rhs, start, stop, r, w):
            S.op("pe", lambda e: e.matmul(out, lhsT=lhsT, rhs=rhs, start=start, stop=stop), r, w)

        def tr(out, in_, r, w):
            S.op("pe", lambda e: e.transpose(out=out, in_=in_, identity=ident[:]), list(r) + ["ident"], w)

        def act(out, in_, func, r, w, **kw):
            S.op("act", lambda e: e.activation(out=out, in_=in_, func=func, **kw), r, w)

        def tt(eng, out, in0, in1, op, r, w):
            S.op(eng, lambda e: e.tensor_tensor(out=out, in0=in0, in1=in1, op=op), r, w)

        def ts2(eng, out, in0, s1, s2, op0, op1, r, w):
            S.op(eng, lambda e: e.tensor_scalar(out=out, in0=in0, scalar1=s1, scalar2=s2, op0=op0, op1=op1), r, w)

        def tsmul(eng, out, in0, s1, r, w):
            S.op(eng, lambda e: e.tensor_scalar_mul(out=out, in0=in0, scalar1=s1), r, w)

        def tsadd(eng, out, in0, s1, r, w):
            S.op(eng, lambda e: e.tensor_scalar_add(out=out, in0=in0, scalar1=s1), r, w)

        def tsmax(eng, out, in0, s1, r, w):
            S.op(eng, lambda e: e.tensor_scalar_max(out=out, in0=in0, scalar1=s1), r, w)

        def stt(eng, out, in0, scalar, in1, op0, op1, r, w):
            S.op(eng, lambda e: e.scalar_tensor_tensor(out=out, in0=in0, scalar=scalar, in1=in1, op0=op0, op1=op1), r, w)

        def cp(eng, out, in_, r, w):
            if eng == "act":
                S.op("act", lambda e: e.copy(out=out, in_=in_), r, w)
            else:
                S.op(eng, lambda e: e.tensor_copy(out=out, in_=in_), r, w)

        def memset(eng, ap, val, w):
            S.op(eng, lambda e: e.memset(ap, val), [], w)

        def dma(q, out, in_, r, w):
            S.op(q, lambda e: e.dma_start(out=out, in_=in_), r, w, dma=True)

        def barrier():
            S.barrier("dve", lambda e: e.memset(DUM[:, 0:1], 0.0))

        PAX = [(PA[0][:], "pa0"), (PA[1][:], "pa1"),
               (PS[0][:].rearrange("p a t -> p (a t)"), "ps0"), (PS[1][:].rearrange("p a t -> p (a t)"), "ps1"),
               (PN[0][:].rearrange("p a t -> p (a t)"), "pn0"), (PN[1][:].rearrange("p a t -> p (a t)"), "pn1")]

        def pax():
            return PAX[nxt("pax", len(PAX))]

        def wslot():
            i = nxt("wb", 3)
            return WB[i], f"WB{i}"

        def wloop(n, src_fn, ncols):
            pend = load_w(src_fn(0), ncols)
            for j in range(n):
                cur = pend
                if j + 1 < n:
                    pend = load_w(src_fn(j + 1), ncols)
                yield j, cur

        def load_w(src_cols_ap, ncols):
            W, key = wslot()
            dma("pool", W[:, :, 0:ncols], src_cols_ap.rearrange("(k p) n -> p k n", p=128), [], [key])
            return W, key

        memset("pool", ident[:], 0.0, ["ident"])
        S.op("pool", lambda e: e.affine_select(out=ident[:], in_=ident[:], pattern=[[-1, 128]], compare_op=ALU.not_equal,
                                               fill=1.0, base=0, channel_multiplier=1), ["ident"], ["ident"])
        memset("pool", MASK4[:], 1.0, ["MASK4"])
        memset("pool", TRI[:], 1.0, ["TRI"])
        for dd in range(2):
            sg = 1 if dd == 0 else -1
            for sl in range(4):
                S.op("pool", lambda e, sl=sl, sg=sg, dd=dd: e.affine_select(
                    out=MASK4[:, dd, sl, :], in_=MASK4[:, dd, sl, :], pattern=[[sg, 128]], compare_op=ALU.is_ge,
                    fill=0.0, base=0, channel_multiplier=-sg), ["MASK4"], ["MASK4"])
        for sl in range(2):
            sg = 1 if sl == 0 else -1
            S.op("pool", lambda e, sl=sl, sg=sg: e.affine_select(
                out=TRI[:, sl, :], in_=TRI[:, sl, :], pattern=[[sg, 128]], compare_op=ALU.is_ge,
                fill=0.0, base=0, channel_multiplier=-sg), ["TRI"], ["TRI"])
        dma("sp", cvF[:], cvecF_d, [], ["cvF"])
        dma("sp", bmodF[:], bmodF_d, [], ["bmodF"])
        dma("sp", n1wF[:], n1wF_d, [], ["n1wF"])
        dma("sp", n2wF[:], n2wF_d, [], ["n2wF"])
        dma("sp", bcvF[:], bcvF_d, [], ["bcvF"])
        dma("sp", bmgF[:], bmgF_d, [], ["bmgF"])
        dma("sp", cwF[:], cwF_d, [], ["cwF"])
        dma("sp", BGB[:], bg_d.partition_broadcast(128), [], ["BGB"])
        dma("pool", WGS[:], wg_d.rearrange("(k p) n -> p k n", p=128), [], ["WGS"])
        tsmul("dve", bmgF[:], bmgF[:], 0.5, ["bmgF"], ["bmgF"])

        act(thF[:], cvF[:], AF.Tanh, ["cvF"], ["thF"], scale=0.5)
        memset("dve", S2[:], 0.0, ["S2"])
        stt("dve", S2[:, :, 0:3], thF[:], 1.0, cvF[:], ALU.add, ALU.mult, ["thF", "cvF", "S2"], ["S2"])

        for blk in (0, 1, 2, 3, 6, 7, 8, 9):
            W, wk = load_w(wmod_d[:, blk * 512:(blk + 1) * 512], 512)
            pi = nxt("pa", 2)
            pa = PA[pi]
            for jj in range(4):
                for k in range(8):
                    mm(pa[:, jj * 4:jj * 4 + 3], W[:, k, jj * 128:(jj + 1) * 128], S2[:, k, 0:3], k == 0, k == 7,
                       [wk, "S2"], [f"pa{pi}"])
            j0 = blk * 4
            stt("dve", modF[:, j0:j0 + 4, :], pa[:, 0:16].rearrange("p (a b) -> p a b", b=4)[:, :, 0:3], 0.5,
                bmodF[:, j0:j0 + 4].unsqueeze(2).to_broadcast([128, 4, 3]), ALU.mult, ALU.add,
                [f"pa{pi}", "bmodF"], ["modF"])
        stt("dve", A1[:], modF[:, 8:16, :], 1.0, n1wF[:].unsqueeze(2).to_broadcast([128, 8, 3]), ALU.add, ALU.mult,
            ["modF", "n1wF"], ["A1"])
        stt("dve", A2[:], modF[:, 32:40, :], 1.0, n2wF[:].unsqueeze(2).to_broadcast([128, 8, 3]), ALU.add, ALU.mult,
            ["modF", "n2wF"], ["A2"])

        SSB = sb("SSB", [128, 2, NTC], F32)

        def sumsq_tile(src, src_keys, i):
            if i % 2 == 0:
                act(XN[0][:], src, AF.Square, list(src_keys), ["XN0", f"SSB{i}"], accum_out=SSB[:, 0, i:i + 1])
            else:
                S.op("dve", lambda e: e.scalar_tensor_tensor(out=XN[1][:], in0=src, scalar=1.0, in1=src, op0=ALU.mult, op1=ALU.mult,
                                                             accum_out=SSB[:, 0, i:i + 1]), list(src_keys), ["XN1", f"SSB{i}"])

        def rstd_all(n):
            ks = [f"SSB{i}" for i in range(n)]
            ts2("dve", SSB[:, 1, 0:n], SSB[:, 0, 0:n], 1.0 / D, EPS, ALU.mult, ALU.add, ks, ["RSB"])
            act(SSB[:, 1, 0:n], SSB[:, 1, 0:n], AF.Ln, ["RSB"], ["RSB"])
            act(SSB[:, 1, 0:n], SSB[:, 1, 0:n], AF.Exp, ["RSB"], ["RSB"], scale=-0.5)

        def norm_apply_T(src, src_keys, i, dst, dst_key, tok0, Avec, SHvec, vec_keys):
            xn = XN[0]
            act(xn[:], src, AF.Copy, list(src_keys) + ["RSB"], ["XN0"], scale=SSB[:, 1, i:i + 1])
            pi = nxt("pt", 2)
            pt = PT[pi]
            for k in range(8):
                tr(pt[:, k, :], xn[:, k * 128:(k + 1) * 128], ["XN0"], [f"pt{pi}"])
            for k in range(8):
                if k % 4 == 3:
                    act(dst[:, k, tok0:tok0 + 128], pt[:, k, :], AF.Identity, [f"pt{pi}"] + list(vec_keys), [dst_key],
                        scale=Avec[:, k:k + 1], bias=SHvec[:, k:k + 1])
                else:
                    ts2("dve", dst[:, k, tok0:tok0 + 128], pt[:, k, :], Avec[:, k:k + 1], SHvec[:, k:k + 1], ALU.mult, ALU.add,
                        [f"pt{pi}"] + list(vec_keys), [dst_key])

        def stage(n):
            if n > kstop:
                raise _Stop()

        try:
          for b in range(NB):
              barrier()
              stage(1)
              def xsrc(i):
                  return x_d[b, i * 128:(i + 1) * 128, :] if i < NT else ctx_d[b, (i - NT) * 128:(i - NT + 1) * 128, :]
              for i in range(NTC):
                  dma("sp", XS[i], xsrc(i), [], [f"XS{i}"])
              for i in range(NTC):
                  sumsq_tile(XS[i], [f"XS{i}"], i)
              rstd_all(NTC)
              for i in range(NTC):
                  r = b if i < NT else 2
                  norm_apply_T(XS[i], [f"XS{i}"], i, hT, "hT", i * 128, A1[:, :, r], modF[:, 0:8, r], ["A1", "modF"])
              stage(2)
              for i in range(NTC):
                  pi = 0 if i < NT else 1
                  sl = i % NT
                  for k in range(8):
                      mm(PA[pi][:, sl * 32:(sl + 1) * 32], hT[:, k, i * 128:(i + 1) * 128], WGS[:, k, :], k == 0, k == 7,
                         ["hT", "WGS"], [f"pa{pi}"])
              tt("dve", G[:, 0:NT, :], PA[0][:].rearrange("p (c w) -> p c w", w=32), BGB[:].unsqueeze(1).to_broadcast([128, NT, 32]),
                 ALU.add, ["pa0", "BGB"], ["G"])
              tt("dve", G[:, NT:NTC, :], PA[1][:, 0:64].rearrange("p (c w) -> p c w", w=32),
                 BGB[:].unsqueeze(1).to_broadcast([128, 2, 32]), ALU.add, ["pa1", "BGB"], ["G"])
              rot["pa"] = 0
              act(NLF[:], G[:, :, 16:32], AF.Exp, ["G"], ["NLF"], scale=-1.0)
              tsadd("dve", NLF[:], NLF[:], 1.0, ["NLF"], ["NLF"])
              act(NLF[:], NLF[:], AF.Ln, ["NLF"], ["NLF"])
              for d in range(2):
                  def g3(nm):
                      return gt[(nm, d)][:].rearrange("p (c h) -> p c h", h=8)
                  cp("dve", g3("TMP"), NLF[:, :, d * 8:(d + 1) * 8], ["NLF"], [f"TMP{d}"])
                  mm(PA[0][:, 0:144], TRI[:, d, :], gt[("TMP", d)][:], True, True, [f"TMP{d}", "TRI"], ["pa0"])
                  mm(PA[1][:, 0:144], TRI[:, 2, :], gt[("TMP", d)][:], True, True, [f"TMP{d}", "TRI"], ["pa1"])
                  cp("act", gt[("NB", d)][:], PA[0][:, 0:144], ["pa0"], [f"NB{d}"])
                  cp("act", gt[("NBL", d)][:], PA[1][:, 0:144], ["pa1"], [f"NBL{d}"])
                  act(gt[("EQ", d)][:], gt[("NB", d)][:], AF.Exp, [f"NB{d}"], [f"EQ{d}"], scale=-1.0)
                  tt("dve", g3("TMP"), G[:, :, d * 8:(d + 1) * 8], g3("NB"), ALU.add, ["G", f"NB{d}"], [f"TMP{d}"])
                  act(gt[("EK", d)][:], gt[("TMP", d)][:], AF.Exp, [f"TMP{d}"], [f"EK{d}"])
                  tsmul("dve", gt[("EK", d)][:], gt[("EK", d)][:], 0.125, [f"EK{d}"], [f"EK{d}"])
                  tt("dve", gt[("TMP", d)][:], gt[("TMP", d)][:], gt[("NBL", d)][:], ALU.subtract, [f"TMP{d}", f"NBL{d}"], [f"TMP{d}"])
                  act(gt[("EKH", d)][:], gt[("TMP", d)][:], AF.Exp, [f"TMP{d}"], [f"EKH{d}"])
                  tsmul("dve", gt[("EKH", d)][:], gt[("EKH", d)][:], 0.125, [f"EKH{d}"], [f"EKH{d}"])
                  act(gt[("DEC", d)][:], gt[("NBL", d)][:], AF.Exp, [f"NBL{d}"], [f"DEC{d}"], scale=-1.0)

              cp("dve", S2rep, S2[:, :, b:b + 1].to_broadcast([128, 8, 128]), ["S2"], ["S2rep"])
              for (blk0, dst, dkey, c_ps, c_b) in ((4, G1H, "G1H", 0.25, 0.5), (10, G2, "G2", 0.5, 1.0)):
                  fi = nxt("f4k", 2)
                  bb = F4K[fi]
                  dma("sp", bb[:], bmod_d[blk0 * 512:blk0 * 512 + 1024].partition_broadcast(128), [], [f"F4K{fi}"])
                  for hb in range(2):
                      W, wk = load_w(wmod_d[:, (blk0 + hb) * 512:(blk0 + hb + 1) * 512], 512)
                      pi = nxt("pa", 2)
                      pa = PA[pi]
                      for k in range(8):
                          mm(pa[:], S2rep[:, k, :], W[:, k, :], k == 0, k == 7, [wk, "S2rep"], [f"pa{pi}"])
                      tsmul("dve", dst[:, hb * 512:(hb + 1) * 512], pa[:], c_ps, [f"pa{pi}"], [dkey])
                      stt("dve", dst[:, hb * 512:(hb + 1) * 512], bb[:, hb * 512:(hb + 1) * 512], c_b,
                          dst[:, hb * 512:(hb + 1) * 512], ALU.mult, ALU.add, [f"F4K{fi}", dkey], [dkey])

              stage(3)
              orders = ([16, 17] + list(range(16)), [17, 16] + list(range(15, -1, -1)))
              tmk = [f"TM{i}" for i in range(NTC)]
              S.op("pool", lambda e: e.memset(TM[:, :, 384:385], 1.0), ["hT"], ["TM1"])
              pending = []

              def build_post(h, mi):
                  P = []
                  P.append(lambda: stt("dve", AD[:], HN[:, :, :, 128], -1.0, HN[:, :, :, 128], ALU.mult, ALU.max, ["HN"], ["AD"]))
                  P.append(lambda: tsmax("dve", AD[:], AD[:], 1.0, ["AD"], ["AD"]))
                  P.append(lambda: S.op("dve", lambda e: e.reciprocal(out=RD[:], in_=AD[:]), ["AD"], ["RD"]))
                  P.append(lambda: tt("dve", HS[:], HN[:, :, 0, 0:128], RD[:, :, 0:1].to_broadcast([128, NT, 128]), ALU.mult,
                                      ["HN", "RD"], ["HS", "KQT"]))
                  P.append(lambda: tt("dve", SQ[:], HN[:, :, 1, 0:128], RD[:, :, 1:2].to_broadcast([128, NT, 128]), ALU.mult,
                                      ["HN", "RD"], ["SQ", "CSf0", "CSf1"]))
                  P.append(lambda: tt("dve", HS[:], HS[:], SQ[:], ALU.add, ["HS", "SQ"], ["HS", "KQT"]))
                  P.append(lambda: tt("dve", SQ[:], HS[:], HS[:], ALU.mult, ["HS"], ["SQ", "CSf0", "CSf1"]))
                  P.append(lambda: S.op("dve", lambda e: e.tensor_reduce(out=SSQ[:], in_=SQ[:], axis=AX.X, op=ALU.add), ["SQ"], ["SSQ"]))
                  P.append(lambda: ts2("dve", SSQ[:], SSQ[:], 1.0 / 128, EPS, ALU.mult, ALU.add, ["SSQ"], ["SSQ"]))
                  P.append(lambda: act(SSQ[:], SSQ[:], AF.Ln, ["SSQ"], ["SSQ"]))
                  P.append(lambda: act(SSQ[:], SSQ[:], AF.Exp, ["SSQ"], ["SSQ"], scale=-0.5))
                  P.append(lambda: tt("dve", HS[:], HS[:], SSQ[:].unsqueeze(2).to_broadcast([128, NT, 128]), ALU.mult,
                                      ["HS", "SSQ"], ["HS", "KQT"]))
                  P.append(lambda: tt("dve", HS[:], HS[:], MW[mi][:].unsqueeze(1).to_broadcast([128, NT, 128]), ALU.mult,
                                      ["HS", f"MW{mi}"], ["HS", "KQT"]))
                  P.append(lambda: stt("dve", GT[:], TO[:], 1.0, HS[:], ALU.add, ALU.mult, ["TO", "HS"], ["GT", "CSf0", "CSf1", "CS"]))

                  def trs(i0):
                      pi = nxt("pt", 2)
                      for ii in range(8):
                          tr(PT[pi][:, ii, :], GT[:, i0 + ii, :], ["GT"], [f"pt{pi}"])
                      cp("act", hsT[:, h, i0 * 128:(i0 + 8) * 128].rearrange("p (a t) -> p a t", t=128), PT[pi][:],
                         [f"pt{pi}"], ["hsT"])
                  P.append(lambda: trs(0))
                  P.append(lambda: trs(8))
                  return P

              for h, (W, wk) in wloop(H, lambda hh: wtm_d[:, hh * 384:(hh + 1) * 384], 384):
                  if h >= 1:
                      stage(4)
                  bi = nxt("bt", 1)
                  dma("sp", BT[bi][:], btm_d[h * 384:(h + 1) * 384].partition_broadcast(128), [], [f"BT{bi}"])
                  mi = nxt("mw", 2)
                  dma("sp", MW[mi][:], mnw_d[h * 128:(h + 1) * 128].partition_broadcast(128), [], [f"MW{mi}"])
                  tsmul("dve", MW[mi][:], MW[mi][:], 0.5, [f"MW{mi}"], [f"MW{mi}"])
                  for i in range(NTC):
                      pa_ap, pa_k = PAX[nxt("pax4", 4)]
                      for k in range(8):
                          mm(pa_ap[:, 0:384], hT[:, k, i * 128:(i + 1) * 128], W[:, k, 0:384], k == 0, k == 7,
                             ["hT", wk], [pa_k])
                      tt("dve", TM[:, i, 0:384], pa_ap[:, 0:384], BT[bi][:], ALU.add, [pa_k, f"BT{bi}"], [f"TM{i}"])
                      if len(pending) > 2:
                          pending.pop(0)()
                  while pending:
                      pending.pop(0)()
                  stage(3.1)
                  def g3d(nm, d):
                      return gt[(nm, d)][:].rearrange("p (c h) -> p c h", h=8)
                  for d in range(2):
                      se = "pool" if d == 0 else "dve"
                      tt(se, KH[:, :, d, :], TM[:, :, 0:64], g3d("EKH", d)[:, :, h:h + 1].to_broadcast([128, NTC, 64]), ALU.mult,
                         tmk + [f"EKH{d}"], [f"KH{d}", "HN"])
                  for d in range(2):
                      se = "pool" if d == 0 else "dve"
                      tt(se, SCL[:, :, d, :], TM[:, :, 0:64], g3d("EK", d)[:, :, h:h + 1].to_broadcast([128, NTC, 64]), ALU.mult,
                         tmk + [f"EK{d}"], [f"SCL{d}", "HN"])
                  for d in range(2):
                      tt("dve", SCL[:, 0:NT, 2 + d, :], TM[:, 0:NT, 64:128], g3d("EQ", d)[:, 0:NT, h:h + 1].to_broadcast([128, NT, 64]),
                         ALU.mult, tmk + [f"EQ{d}"], [f"SCLQ{d}", "HN"])
                  stage(3.3)
                  for i0 in range(0, NTC, 2):
                      pi = nxt("pn", 2)
                      for ii in range(2):
                          i = i0 + ii
                          mm(PN[pi][:, ii, 0:129], KH[:, i].rearrange("p a b -> p (a b)"), TM[:, i, 256:385], True, True,
                             ["KH0", "KH1", f"TM{i}", "TM1"], [f"pn{pi}"])
                      cp("act", DC[:, i0:i0 + 2, :], PN[pi][:, :, 0:129], [f"pn{pi}"], ["DC", "HN", "TO"])
                  stage(3.4)
                  decs = [gt[("DEC", d)][:].rearrange("p (c h) -> p c h", h=8) for d in range(2)]
                  for d in range(2):
                      rows = slice(d * 64, (d + 1) * 64)
                      memset("dve", CSf[rows, orders[d][0], :], 0.0, [f"CSf{d}", "SQ", "GT"])
                  for kk in range(1, NTC):
                      for d in range(2):
                          rows = slice(d * 64, (d + 1) * 64)
                          c_prev, c = orders[d][kk - 1], orders[d][kk]
                          stt("dve", CSf[rows, c, :], CSf[rows, c_prev, :], decs[d][rows, c_prev, h:h + 1], DC[rows, c_prev, :],
                              ALU.mult, ALU.add, [f"CSf{d}", f"DEC{d}", "DC"], [f"CSf{d}"])
                  stage(3.2)
                  for i0 in range(0, NT, 4):
                      pi = nxt("pt", 2)
                      for ii in range(4):
                          i = i0 + ii
                          tr(PT[pi][:, 2 * ii, :], SCL[:, i, 0:2, :].rearrange("p a b -> p (a b)"), ["SCL0", "SCL1"], [f"pt{pi}"])
                          tr(PT[pi][:, 2 * ii + 1, :], SCL[:, i, 2:4, :].rearrange("p a b -> p (a b)"), ["SCLQ0", "SCLQ1"], [f"pt{pi}"])
                      cp("act", KQT[:, i0:i0 + 4].rearrange("p c a t -> p (c a) t"), PT[pi][:], [f"pt{pi}"], ["KQT", "HS"])
                  stage(3.45)
                  cp("act", CS[:, :, 0:129], CSf[:, 0:NT, :], ["CSf0", "CSf1"], ["CS", "GT"])
                  act(TO[:], TM[:, 0:NT, 128:256], AF.Tanh, tmk, ["TO", "DC"], scale=0.5)
                  stage(3.5)
                  def scores(i0):
                      for ii in range(4):
                          i = i0 + ii
                          for d in range(2):
                              rows = slice(d * 64, (d + 1) * 64)
                              mm(PS[d][:, ii, :], KQT[rows, i, 0, :], KQT[rows, i, 1, :], True, True, ["KQT"], [f"ps{d}"])
                      pq = nxt("pp", 2)
                      for d in range(2):
                          tt("dve", PP[pq * 2 + d][:], PS[d][:], MASK4[:, d], ALU.mult, [f"ps{d}", "MASK4"], [f"PP{pq * 2 + d}"])
                      return pq

                  def outputs(i0, pq):
                      for ii in range(4):
                          i = i0 + ii
                          pi = nxt("pn", 2)
                          for d in range(2):
                              rows = slice(d * 64, (d + 1) * 64)
                              mm(PN[pi][:, d, 0:129], PP[pq * 2 + d][:, ii, :], TM[:, i, 256:385], True, False,
                                 [f"PP{pq * 2 + d}", f"TM{i}", "TM1"], [f"pn{pi}"])
                              mm(PN[pi][:, d, 0:129], KQT[rows, i, 1, :], CS[rows, i, 0:129], False, True,
                                 ["KQT", "CS"], [f"pn{pi}"])
                          cp("act", HN[:, i, :, :], PN[pi][:, :, 0:129], [f"pn{pi}"],
                             ["HN", "SCL0", "SCL1", "SCLQ0", "SCLQ1", "KH0", "KH1", "DC"])

                  pq_prev = scores(0)
                  for i0 in range(0, NT, 4):
                      pq_next = scores(i0 + 4) if i0 + 4 < NT else None
                      outputs(i0, pq_prev)
                      pq_prev = pq_next
                  stage(3.6)
                  pending = build_post(h, mi)
              while pending:
                  pending.pop(0)()

              stage(5)
              barrier()
              for c, (W, wk) in wloop(8, lambda cc: wcv_d[:, cc * 384:(cc + 1) * 384], 384):
                  for g in range(4):
                      tsl = slice(g * 512, (g + 1) * 512)
                      for part in range(3):
                          pa_ap, pa_k = pax()
                          for k in range(8):
                              mm(pa_ap, W[:, k, part * 128:(part + 1) * 128], hT[:, k, tsl], k == 0, k == 7, ["hT", wk], [pa_k])
                          act(X3[part], pa_ap, AF.Identity, [pa_k, "bcvF"], [f"X3{part}"],
                              bias=bcvF[:, c * 3 + part:c * 3 + part + 1])
                      tt("pool", CU, X3[1], X3[0], ALU.mult, ["X31", "X30"], ["CU"])
                      tsmul("dve", CA, CU, cwF[:, c, 1:2], ["CU", "cwF"], ["CA"])
                      U3 = CU.rearrange("p (r w) -> p r w", w=64)
                      A3 = CA.rearrange("p (r w) -> p r w", w=64)
                      stt("dve", A3[:, :, 1:64], U3[:, :, 0:63], cwF[:, c, 0:1], A3[:, :, 1:64], ALU.mult, ALU.add, ["CU", "CA", "cwF"], ["CA"])
                      stt("dve", A3[:, :, 0:63], U3[:, :, 1:64], cwF[:, c, 2:3], A3[:, :, 0:63], ALU.mult, ALU.add, ["CU", "CA", "cwF"], ["CA"])
                      tt("pool", aT[:, c, tsl], CA, X3[2], ALU.mult, ["CA", "X32"], ["aT"])

              stage(6)
              barrier()
              for j, (W, wk) in wloop(8, lambda jj: wpost_d[:, jj * 512:(jj + 1) * 512], 512):
                  for g in range(4):
                      tsl = slice(g * 512, (g + 1) * 512)
                      for (which, src, skey, Tt, Mm) in ((0, aT, "aT", T1, M1), (1, hsT, "hsT", T2, M2)):
                          pa_ap, pa_k = pax()
                          for k in range(8):
                              mm(pa_ap, W[:, k, 256 + which * 128:384 + which * 128], hT[:, k, tsl], k == 0, k == 7, ["hT", wk], [pa_k])
                          act(Tt, pa_ap, AF.Tanh, [pa_k, "bmgF"], [f"T{which}"], scale=0.5,
                              bias=bmgF[:, j * 2 + which:j * 2 + which + 1])
                          pa_ap, pa_k = pax()
                          for k in range(8):
                              mm(pa_ap, W[:, k, which * 128:(which + 1) * 128], src[:, k, tsl], k == 0, k == 7, [skey, wk], [pa_k])
                          stt("dve", Mm, Tt, 1.0, pa_ap, ALU.add, ALU.mult, [f"T{which}", pa_k], [f"M{which}"])
                      tt("pool", mT[:, j, tsl], M1, M2, ALU.add, ["M0", "M1"], ["mT"])

              stage(7)
              barrier()
              for k in range(8):
                  fi = nxt("f4k", 2)
                  dma("sp", F4K[fi][:], wout_d[k * 128:(k + 1) * 128, :], [], [f"F4K{fi}"])
                  tt("dve", WO[:, k, :], F4K[fi][:], G1H[:], ALU.mult, [f"F4K{fi}", "G1H"], ["WO"])
              for i in range(NT):
                  dma("sp", X1[:, i, :], x_d[b, i * 128:(i + 1) * 128, :], [], [f"X1_{i}"])
                  for nb in range(2):
                      pa_ap, pa_k = pax()
                      for k in range(8):
                          mm(pa_ap, mT[:, k, i * 128:(i + 1) * 128], WO[:, k, nb * 512:(nb + 1) * 512], k == 0, k == 7, ["mT", "WO"], [pa_k])
                      tt("dve", X1[:, i, nb * 512:(nb + 1) * 512], pa_ap, X1[:, i, nb * 512:(nb + 1) * 512], ALU.add,
                         [pa_k, f"X1_{i}"], [f"X1_{i}"])
                  sumsq_tile(X1[:, i, :], [f"X1_{i}"], i)
              rstd_all(NT)
              barrier()
              for i in range(NT):
                  norm_apply_T(X1[:, i, :], [f"X1_{i}"], i, h2T, "h2T", i * 128, A2[:, :, b], modF[:, 24:32, b], ["A2", "modF"])

              stage(8)
              for qd, (W1, w1k) in wloop(NQ, lambda qq: wff1_d[:, qq * 512:(qq + 1) * 512], 512):
                  par = qd % 2
                  for fl in range(FQ):
                      fi = nxt("f4k", 2)
                      f = qd * FQ + fl
                      dma("sp", F4K[fi][:], wff2_d[f * 128:(f + 1) * 128, :], [], [f"F4K{fi}"])
                      tt("pool", W2S[par][:, fl, :], F4K[fi][:], G2[:], ALU.mult, [f"F4K{fi}", "G2"], [f"W2S{par}"])
                  for fl in range(FQ):
                      for g in range(4):
                          tsl = slice(g * 512, (g + 1) * 512)
                          pa_ap, pa_k = pax()
                          for k in range(8):
                              mm(pa_ap, W1[:, k, fl * 128:(fl + 1) * 128], h2T[:, k, tsl], k == 0, k == 7, ["h2T", w1k], [pa_k])
                          ri = nxt("rr", 2)
                          act(RR[ri], pa_ap, AF.Relu, [pa_k], [f"RR{ri}"])
                          tt("pool", AQ[:, fl, tsl], RR[ri], RR[ri], ALU.mult, [f"RR{ri}"], ["AQ"])
                  for i in range(NT):
                      for nb in range(2):
                          pa_ap, pa_k = pax()
                          for fl in range(FQ):
                              mm(pa_ap, AQ[:, fl, i * 128:(i + 1) * 128], W2S[par][:, fl, nb * 512:(nb + 1) * 512], fl == 0, fl == FQ - 1,
                                 ["AQ", f"W2S{par}"], [pa_k])
                          tt("dve", X1[:, i, nb * 512:(nb + 1) * 512], pa_ap, X1[:, i, nb * 512:(nb + 1) * 512], ALU.add,
                             [pa_k, f"X1_{i}"], [f"X1_{i}"])
                      if qd == NQ - 1:
                          sumsq_tile(X1[:, i, :], [f"X1_{i}"], i)

              stage(9)
              barrier()
              dma("sp", FNWB, fnw_d.partition_broadcast(128), [], ["FNWB"])
              rstd_all(NT)
              for i in range(NT):
                  fi = nxt("f4k", 2)
                  stt("dve", F4K[fi][:], X1[:, i, :], SSB[:, 1, i:i + 1], FNWB, ALU.mult, ALU.mult, [f"X1_{i}", "RSB", "FNWB"], [f"F4K{fi}"])
                  dma("sp", out_d[b, i * 128:(i + 1) * 128, :], F4K[fi][:], [f"F4K{fi}"], [])
        except _Stop:
            pass

        S.emit({"pe": block.tensor, "act": block.scalar, "dve": block.vector, "pool": block.gpsimd, "sp": block.sync}, sems, dsems)
    return nc


def _fm(v, nchunk):
    return np.ascontiguousarray(np.asarray(v, np.float32).reshape(nchunk, 128).T)


def kernel(x, c, ctx, c_ctx, w_mod, b_mod, norm1_w, w_in, b_in, conv_w, mlstm_norm_w,
           w_conv_out, w_mlstm_out, w_out, norm2_w, w_ff1, w_ff2, final_norm_w):
    f32 = np.float32
    x = np.asarray(x, f32)
    c = np.asarray(c, f32)
    ctx = np.asarray(ctx, f32)
    c_ctx = np.asarray(c_ctx, f32)
    w_in0 = np.asarray(w_in, f32)[0]
    b_in0 = np.asarray(b_in, f32)[0]
    B = x.shape[0]
    NB = B // N_CORES

    o_k, o_v, o_ig, o_fg, o_q, o_o = 0, 512, 1536, 1552, 1568, 2080
    o_xin, o_gc, o_gb, o_mg = 3104, 4128, 5152, 6176
    cols_tm, cols_cv, cols_mg = [], [], []
    for h in range(H):
        cols_tm += list(range(o_k + h * 64, o_k + (h + 1) * 64))
        cols_tm += list(range(o_q + h * 64, o_q + (h + 1) * 64))
        cols_tm += list(range(o_o + h * 128, o_o + (h + 1) * 128))
        cols_tm += list(range(o_v + h * 128, o_v + (h + 1) * 128))
    for ch in range(8):
        for base in (o_xin, o_gc, o_gb):
            cols_cv += list(range(base + ch * 128, base + (ch + 1) * 128))
    cols_g = list(range(o_ig, o_ig + 16)) + list(range(o_fg, o_fg + 16))
    wco = np.asarray(w_conv_out, f32)[0]
    wmo = np.asarray(w_mlstm_out, f32)[0]
    wpost = np.concatenate(
        [np.concatenate([wco[:, j * 128:(j + 1) * 128], wmo[:, j * 128:(j + 1) * 128],
                         w_in0[:, o_mg + j * 128:o_mg + (j + 1) * 128],
                         w_in0[:, o_mg + 1024 + j * 128:o_mg + 1024 + (j + 1) * 128]], axis=1) for j in range(8)], axis=1)
    bcvF = np.stack([b_in0[base + ch * 128:base + (ch + 1) * 128] for ch in range(8) for base in (o_xin, o_gc, o_gb)], axis=1)
    bmgF = np.stack([b_in0[o_mg + w * 1024 + j * 128:o_mg + w * 1024 + (j + 1) * 128] for j in range(8) for w in range(2)], axis=1)
    shared = {
        "wmod": np.ascontiguousarray(np.asarray(w_mod, f32)[0]),
        "bmodF": _fm(np.asarray(b_mod, f32)[0], 48),
        "bmod": np.ascontiguousarray(np.asarray(b_mod, f32)[0]),
        "n1wF": _fm(np.asarray(norm1_w, f32)[0], 8),
        "n2wF": _fm(np.asarray(norm2_w, f32)[0], 8),
        "wg": np.ascontiguousarray(w_in0[:, cols_g]),
        "wtm": np.ascontiguousarray(w_in0[:, cols_tm]),
        "wcv": np.ascontiguousarray(w_in0[:, cols_cv]),
        "wpost": np.ascontiguousarray(wpost),
        "bg": np.ascontiguousarray(b_in0[cols_g]),
        "btm": np.ascontiguousarray(b_in0[cols_tm]),
        "bcvF": np.ascontiguousarray(bcvF),
        "bmgF": np.ascontiguousarray(bmgF),
        "cwF": np.ascontiguousarray(np.asarray(conv_w, f32)[0].reshape(3, 8, 128).transpose(2, 1, 0)),
        "mnw": np.ascontiguousarray(np.asarray(mlstm_norm_w, f32)[0]),
        "wout": np.ascontiguousarray(np.asarray(w_out, f32)[0]),
        "wff1": np.ascontiguousarray(np.asarray(w_ff1, f32)[0]),
        "wff2": np.ascontiguousarray(np.asarray(w_ff2, f32)[0]),
        "fnw": np.ascontiguousarray(np.asarray(final_norm_w, f32)),
    }
    in_maps = []
    for core in range(N_CORES):
        bs = slice(core * NB, (core + 1) * NB)
        cv = np.concatenate([c[bs], c_ctx[None, :]], axis=0)
        assert NB == 2
        m = dict(shared)
        m["x"] = np.ascontiguousarray(x[bs])
        m["ctx"] = np.ascontiguousarray(ctx[bs])
        m["cvecF"] = np.ascontiguousarray(cv.reshape(3, 8, 128).transpose(2, 1, 0))
        in_maps.append(m)
    nc = build_program(NB)
    res = run_bass_kernel_spmd(nc, in_maps, core_ids=list(range(N_CORES)))
    out = np.concatenate([np.asarray(r["out"], f32) for r in res.results], axis=0)
    return out
```
